# Optimizing a Trainium2 kernel written in Bass

```python
import jax, jax.numpy as jnp
from jax import lax
import numpy as np

D_MODEL = 1024
BATCH = 32
SEQ = 2048
DEPTH = 2

GRID_W = 64
CTX_LEN = 256
HEAD_DIM = 64
ATTN_W = D_MODEL // 2
CONV_W = D_MODEL // 4
POOL_W = D_MODEL // 4
MIX_W = ATTN_W + CONV_W + POOL_W
ATTN_HEADS = ATTN_W // HEAD_DIM
KV_HEADS = ATTN_HEADS // 4
KV_W = KV_HEADS * HEAD_DIM
IN_W = ATTN_W + 2 * KV_W + 2 * CONV_W + POOL_W
WINDOW = 128
Q_BLOCK = 128
SPAN = Q_BLOCK + 2 * WINDOW
CONV_KERNEL = 31
POOL_WINDOWS = (2, 4, 8, 16)
POOL_GROUP = POOL_W // len(POOL_WINDOWS)
ROPE_BASE = 10000.0
D_FF = -(-8 * D_MODEL // (3 * 256)) * 256
EPS = 1e-6
NEG = -1e30

kernel_name = "hybrid_parallel_groups_dit_block"


def rms_norm(x, g):
    xf = x.astype(jnp.float32)
    y = xf * lax.rsqrt(jnp.mean(xf * xf, axis=-1, keepdims=True) + EPS)
    return (y * g.astype(jnp.float32)).astype(x.dtype)


def axial_rope_tables(n, dtype):
    rows = n // GRID_W
    row = jnp.repeat(jnp.arange(rows), GRID_W).astype(jnp.float32)
    col = jnp.tile(jnp.arange(GRID_W), rows).astype(jnp.float32)
    half = HEAD_DIM // 2
    inv = ROPE_BASE ** (-jnp.arange(0, half, 2, dtype=jnp.float32) / half)
    ar = row[:, None] * inv
    ac = col[:, None] * inv
    ang = jnp.concatenate([ar, ar, ac, ac], axis=-1)
    return jnp.cos(ang).astype(dtype), jnp.sin(ang).astype(dtype)


def apply_rope(x, cos, sin):
    xr = x.reshape(*x.shape[:-1], 2, 2, HEAD_DIM // 4)
    rot = jnp.stack([-xr[..., 1, :], xr[..., 0, :]], axis=-2).reshape(x.shape)
    return x * cos[:, None, :] + rot * sin[:, None, :]


def split_in(u):
    b, n, _ = u.shape
    q, k, v, cu, pu = jnp.split(
        u, [ATTN_W, ATTN_W + KV_W, ATTN_W + 2 * KV_W, ATTN_W + 2 * KV_W + 2 * CONV_W], axis=-1)
    return (q.reshape(b, n, ATTN_HEADS, HEAD_DIM), k.reshape(b, n, KV_HEADS, HEAD_DIM),
            v.reshape(b, n, KV_HEADS, HEAD_DIM), cu, pu)


def window_attention(q, k, v, k_ctx, v_ctx, sink):
    b, n, h, hd = q.shape
    kvh = k.shape[2]
    grp = h // kvh
    n_ctx = k_ctx.shape[1]
    nb = n // Q_BLOCK
    scale = HEAD_DIM ** -0.5
    pad = ((0, 0), (WINDOW, WINDOW), (0, 0), (0, 0))
    k_pad = jnp.pad(k, pad)
    v_pad = jnp.pad(v, pad)
    sink_b = jnp.broadcast_to(sink.astype(jnp.float32).reshape(1, kvh, grp, 1, 1), (b, kvh, grp, Q_BLOCK, 1))

    def one_block(i):
        start = i * Q_BLOCK
        qb = lax.dynamic_slice_in_dim(q, start, Q_BLOCK, axis=1).reshape(b, Q_BLOCK, kvh, grp, hd)
        kb = lax.dynamic_slice_in_dim(k_pad, start, SPAN, axis=1)
        vb = lax.dynamic_slice_in_dim(v_pad, start, SPAN, axis=1)
        qpos = start + jnp.arange(Q_BLOCK)
        kpos = start - WINDOW + jnp.arange(SPAN)
        valid = ((jnp.abs(qpos[:, None] - kpos[None, :]) <= WINDOW)
                 & (kpos >= 0)[None, :] & (kpos < n)[None, :])
        s_loc = jnp.einsum('bqkgd,bjkd->bkgqj', qb, kb).astype(jnp.float32) * scale
        s_loc = jnp.where(valid, s_loc, NEG)
        s_ctx = jnp.einsum('bqkgd,bckd->bkgqc', qb, k_ctx).astype(jnp.float32) * scale
        p = jax.nn.softmax(jnp.concatenate([sink_b, s_ctx, s_loc], axis=-1), axis=-1).astype(v.dtype)
        o = (jnp.einsum('bkgqc,bckd->bqkgd', p[..., 1:1 + n_ctx], v_ctx)
             + jnp.einsum('bkgqj,bjkd->bqkgd', p[..., 1 + n_ctx:], vb))
        return o.reshape(b, Q_BLOCK, h * hd)

    o = lax.map(one_block, jnp.arange(nb))
    return jnp.moveaxis(o, 0, 1).reshape(b, n, h * hd)


def context_attention(qc, kc, vc, sink):
    b, n_ctx, h, hd = qc.shape
    kvh = kc.shape[2]
    grp = h // kvh
    qg = qc.reshape(b, n_ctx, kvh, grp, hd)
    s = jnp.einsum('bqkgd,bckd->bkgqc', qg, kc).astype(jnp.float32) * (HEAD_DIM ** -0.5)
    sink_b = jnp.broadcast_to(sink.astype(jnp.float32).reshape(1, kvh, grp, 1, 1), (b, kvh, grp, n_ctx, 1))
    p = jax.nn.softmax(jnp.concatenate([sink_b, s], axis=-1), axis=-1).astype(vc.dtype)
    o = jnp.einsum('bkgqc,bckd->bqkgd', p[..., 1:], vc)
    return o.reshape(b, n_ctx, h * hd)


def conv_module(u, dw, dw_b, ln_g, ln_b):
    a, g = jnp.split(u, 2, axis=-1)
    h = a * jax.nn.sigmoid(g)
    h = lax.conv_general_dilated(
        h, dw[:, None, :], window_strides=(1,),
        padding=[(CONV_KERNEL // 2, CONV_KERNEL // 2)],
        dimension_numbers=('NWC', 'WIO', 'NWC'), feature_group_count=CONV_W) + dw_b
    hf = h.astype(jnp.float32)
    mu = jnp.mean(hf, axis=-1, keepdims=True)
    var = jnp.mean(jnp.square(hf - mu), axis=-1, keepdims=True)
    hn = (hf - mu) * lax.rsqrt(var + EPS) * ln_g.astype(jnp.float32) + ln_b.astype(jnp.float32)
    return jax.nn.silu(hn).astype(u.dtype)


def pool_mixer(p, w, scale):
    b, n, ch = p.shape
    t = jnp.arange(n)
    pf = p.astype(jnp.float32).reshape(b, n, len(POOL_WINDOWS), POOL_GROUP)
    cs = jnp.pad(jnp.cumsum(pf, axis=1), ((0, 0), (1, 0), (0, 0), (0, 0)))
    outs = []
    for gi, win in enumerate(POOL_WINDOWS):
        lo = jnp.maximum(t - win // 2, 0)
        hi = jnp.minimum(t + win - 1 - win // 2, n - 1)
        cg = cs[:, :, gi]
        mean = (cg[:, hi + 1] - cg[:, lo]) / (hi - lo + 1).astype(jnp.float32)[None, :, None]
        outs.append(mean - pf[:, :, gi])
    y = jnp.stack(outs, axis=2).astype(p.dtype)
    y = jnp.einsum('bsgc,gcd->bsgd', y, w).reshape(b, n, ch)
    return y * scale


def mixer_output(attn, cu, pu, w_out, dw, dw_b, ln_g, ln_b, pw, ps):
    conv = conv_module(cu, dw, dw_b, ln_g, ln_b)
    pool = pool_mixer(pu, pw, ps)
    return jnp.concatenate([attn, conv, pool], axis=-1) @ w_out


def swiglu(h, w_in, w_out):
    g, u = jnp.split(h @ w_in, 2, axis=-1)
    return (jax.nn.silu(g) * u) @ w_out


def setup_inputs(seed: int = 0) -> dict:
    key = jax.random.key(seed)
    ks = jax.random.split(key, 24)
    f32 = jnp.float32
    nrm = lambda k, shape, s: jax.random.normal(k, shape, f32) * s
    return {
        "x": nrm(ks[0], (BATCH, SEQ, D_MODEL), 1.0),
        "c": nrm(ks[1], (BATCH, D_MODEL), 1.0),
        "ctx": nrm(ks[2], (BATCH, CTX_LEN, D_MODEL), 1.0),
        "c_ctx": nrm(ks[3], (D_MODEL,), 1.0),
        "w_mod": nrm(ks[4], (DEPTH, D_MODEL, 6 * D_MODEL), 0.5 * D_MODEL ** -0.5),
        "b_mod": nrm(ks[5], (DEPTH, 6 * D_MODEL), 0.01),
        "norm1_g": 1.0 + nrm(ks[6], (DEPTH, D_MODEL), 0.05),
        "norm2_g": 1.0 + nrm(ks[7], (DEPTH, D_MODEL), 0.05),
        "w_in": nrm(ks[8], (DEPTH, D_MODEL, IN_W), D_MODEL ** -0.5),
        "conv_dw": nrm(ks[9], (DEPTH, CONV_KERNEL, CONV_W), CONV_KERNEL ** -0.5),
        "conv_dw_b": nrm(ks[10], (DEPTH, CONV_W), 0.01),
        "conv_ln_g": 1.0 + nrm(ks[11], (DEPTH, CONV_W), 0.05),
        "conv_ln_b": nrm(ks[12], (DEPTH, CONV_W), 0.01),
        "attn_sink": nrm(ks[13], (DEPTH, ATTN_HEADS), 0.5),
        "pool_w": nrm(ks[14], (DEPTH, len(POOL_WINDOWS), POOL_GROUP, POOL_GROUP), POOL_GROUP ** -0.5),
        "pool_scale": 1.0 + nrm(ks[15], (DEPTH, POOL_W), 0.05),
        "w_out": nrm(ks[16], (DEPTH, MIX_W, D_MODEL), MIX_W ** -0.5),
        "w_ffn_in": nrm(ks[17], (DEPTH, D_MODEL, 2 * D_FF), D_MODEL ** -0.5),
        "w_ffn_out": nrm(ks[18], (DEPTH, D_FF, D_MODEL), D_FF ** -0.5),
        "final_g": 1.0 + nrm(ks[19], (D_MODEL,), 0.05),
    }


def reference(x, c, ctx, c_ctx, w_mod, b_mod, norm1_g, norm2_g, w_in, conv_dw, conv_dw_b,
              conv_ln_g, conv_ln_b, attn_sink, pool_w, pool_scale, w_out, w_ffn_in, w_ffn_out, final_g):
    b, n, _ = x.shape
    n_ctx = ctx.shape[1]
    cos, sin = axial_rope_tables(n, x.dtype)
    cx = ctx
    for l in range(DEPTH):
        last = l == DEPTH - 1
        m = (jax.nn.silu(c) @ w_mod[l] + b_mod[l])[:, None, :]
        sh1, sc1, g1, sh2, sc2, g2 = jnp.split(m, 6, axis=-1)
        mc = jax.nn.silu(c_ctx) @ w_mod[l] + b_mod[l]
        csh1, csc1, cg1, csh2, csc2, cg2 = jnp.split(mc, 6)

        hl = rms_norm(x, norm1_g[l]) * (1.0 + sc1) + sh1
        hc = rms_norm(cx, norm1_g[l]) * (1.0 + csc1) + csh1
        q, k, v, cu, pu = split_in(hl @ w_in[l])
        if last:
            kvc = hc @ w_in[l][:, ATTN_W:ATTN_W + 2 * KV_W]
            kc, vc = [t.reshape(b, n_ctx, KV_HEADS, HEAD_DIM) for t in jnp.split(kvc, 2, axis=-1)]
        else:
            qc, kc, vc, cuc, puc = split_in(hc @ w_in[l])
        q = apply_rope(q, cos, sin)
        k = apply_rope(k, cos, sin)
        attn = window_attention(q, k, v, kc, vc, attn_sink[l])
        x = x + g1 * mixer_output(attn, cu, pu, w_out[l], conv_dw[l], conv_dw_b[l],
                                  conv_ln_g[l], conv_ln_b[l], pool_w[l], pool_scale[l])
        if not last:
            attn_c = context_attention(qc, kc, vc, attn_sink[l])
            cx = cx + cg1 * mixer_output(attn_c, cuc, puc, w_out[l], conv_dw[l], conv_dw_b[l],
                                         conv_ln_g[l], conv_ln_b[l], pool_w[l], pool_scale[l])

        x = x + g2 * swiglu(rms_norm(x, norm2_g[l]) * (1.0 + sc2) + sh2, w_ffn_in[l], w_ffn_out[l])
        if not last:
            cx = cx + cg2 * swiglu(rms_norm(cx, norm2_g[l]) * (1.0 + csc2) + csh2, w_ffn_in[l], w_ffn_out[l])
    return rms_norm(x, final_g)
```

```python
import contextlib
import numpy as np
import concourse.bass as bass
import concourse.mybir as mybir
from concourse.bass_utils import run_bass_kernel_spmd

F32 = mybir.dt.float32
BF16 = mybir.dt.bfloat16
ALU = mybir.AluOpType
AF = mybir.ActivationFunctionType

D = 1024
S = 2048
LC = 256
DEPTH = 2
NB = 4
DFF = 2816
KF = DFF // 128
EPS = 1e-6
NPIECE = 26
PCOLS = 4096
PIECE_N = [4096, 4096, 3072, 4096, 2048, 4096, 4096] + [4096] * 11 + [2816] * 8
LSP = 142
NSP = 2 * LSP + 8 + 512
NCT = 4096 + 256 + 2 + 32 + 2 + 128
ENGS = ("pe", "act", "dve", "pool", "sp")


class Res:
    __slots__ = ("name", "w", "r")

    def __init__(self, name, legacy):
        self.name = name
        self.w = None
        self.r = dict(legacy)


class Prog:
    def __init__(self, nc):
        self.nc = nc
        self.ins = {e: [] for e in ENGS}
        self.nchan = {}
        self.legacy = {}

    def res(self, name="r"):
        return Res(name, self.legacy)

    def free(self, resources):
        for r in resources:
            if r.w is not None:
                s, i = r.w
                self.legacy[s] = max(self.legacy.get(s, 0), i)
            for s, i in r.r.items():
                self.legacy[s] = max(self.legacy.get(s, 0), i)

    def _deps(self, reads, writes):
        deps = {}
        for r in reads:
            if r.w is not None:
                s, i = r.w
                if deps.get(s, 0) < i:
                    deps[s] = i
        for w in writes:
            if w.w is not None:
                s, i = w.w
                if deps.get(s, 0) < i:
                    deps[s] = i
            for s, i in w.r.items():
                if deps.get(s, 0) < i:
                    deps[s] = i
        return deps

    def op(self, eng, fn, reads=(), writes=()):
        deps = self._deps(reads, writes)
        idx = len(self.ins[eng]) + 1
        self.ins[eng].append((fn, deps, None))
        for r in reads:
            if r.r.get(eng, 0) < idx:
                r.r[eng] = idx
        for w in writes:
            w.w = (eng, idx)
            w.r = {}
        return (eng, idx)

    def dma(self, queue, chan, fn, reads=(), writes=()):
        deps = self._deps(reads, writes)
        k = self.nchan.get(chan, 0) + 1
        self.nchan[chan] = k
        src = "ch:" + chan
        self.ins[queue].append((fn, deps, chan))
        for r in reads:
            if r.r.get(src, 0) < k:
                r.r[src] = k
        for w in writes:
            w.w = (src, k)
            w.r = {}
        return (src, k)

    def wait_tokens(self, eng, tokens):
        deps = {}
        for s, i in tokens:
            deps[s] = max(deps.get(s, 0), i)
        self.ins[eng].append((None, deps, None))

    def emit(self):
        nc = self.nc
        needed = {e: set() for e in ENGS}
        plans = {}
        for e in ENGS:
            last = {}
            plan = []
            for n, (fn, deps, chan) in enumerate(self.ins[e]):
                waits = []
                for s, i in deps.items():
                    if s == e and e == "pe":
                        continue
                    if last.get(s, 0) >= i:
                        continue
                    last[s] = i
                    waits.append((s, i))
                    if not s.startswith("ch:"):
                        needed[s].add(i)
                plan.append(waits)
            plans[e] = plan
        rank = {}
        for e in ENGS:
            srt = sorted(needed[e])
            rank[e] = {i: k + 1 for k, i in enumerate(srt)}
        with contextlib.ExitStack() as st:
            sems = {}
            for e in ENGS:
                sems[e] = st.enter_context(nc.semaphore("s_" + e))
            for c in self.nchan:
                sems["ch:" + c] = st.enter_context(nc.semaphore("c_" + c))
            block = st.enter_context(nc.Block())

            def make(e):
                def body(eng):
                    for n, (fn, deps, chan) in enumerate(self.ins[e]):
                        for s, i in plans[e][n]:
                            if s.startswith("ch:"):
                                eng.wait_ge(sems[s], 16 * i)
                            else:
                                eng.wait_ge(sems[s], rank[s][i])
                        if fn is None:
                            continue
                        bi = fn(eng)
                        if chan is not None:
                            bi.then_inc(sems["ch:" + chan], 16)
                        elif (n + 1) in needed[e]:
                            bi.then_inc(sems[e], 1)
                return body

            block.tensor(make("pe"))
            block.scalar(make("act"))
            block.vector(make("dve"))
            block.gpsimd(make("pool"))
            block.sync(make("sp"))


SB_BASE = 16512
SB_LIMIT = 229344
_ALLOC = {"off": SB_BASE, "n": 0, "peak": 0}
_DTSZ = {}


class Scope:
    def __init__(self, P, nc):
        self.P, self.nc = P, nc
        self.start = _ALLOC["off"]
        self.rs = []

    def sb(self, name, shape, dt):
        sz = 2 if dt == BF16 else 4
        nbytes = int(np.prod(shape[1:])) * sz
        off = (_ALLOC["off"] + 63) // 64 * 64
        assert off + nbytes <= SB_LIMIT, ("SBUF overflow", name, off, nbytes)
        _ALLOC["off"] = off + nbytes
        _ALLOC["peak"] = max(_ALLOC["peak"], off + nbytes)
        _ALLOC["n"] += 1
        return self.nc.alloc_sbuf_tensor_at("%s_%d" % (name, _ALLOC["n"]), list(shape), dt, offset=off)

    def res(self, name="r"):
        r = self.P.res(name)
        self.rs.append(r)
        return r

    def close(self):
        self.P.free(self.rs)
        _ALLOC["off"] = self.start


def _rope_perm():
    perm = np.zeros(64, np.int64)
    sign = np.zeros(64, np.float32)
    for a in range(2):
        for b in range(2):
            for c in range(16):
                d = a * 32 + b * 16 + c
                perm[d] = a * 32 + (1 - b) * 16 + c
                sign[d] = -1.0 if b == 0 else 1.0
    return perm, sign


def _kmajor(w, kc):
    n = w.shape[1]
    return np.ascontiguousarray(w.reshape(kc, 128, n).transpose(1, 0, 2)).reshape(128, kc * n)


def _host_weights(w_in, w_out, w_ffn_in, w_ffn_out):
    perm, _ = _rope_perm()
    out = np.zeros((DEPTH, NPIECE, 128, PCOLS), np.float32)
    ar = np.arange(128)
    hd = ar // 64
    dd = ar % 64
    for l in range(DEPTH):
        w = w_in[l]

        def qc(c):
            return c * 128 + ar

        def qr(c):
            return c * 128 + hd * 64 + perm[dd]
        kcol = 512 + ar
        krcol = 512 + hd * 64 + perm[dd]
        vcol = 640 + ar

        def a_(c):
            return 768 + c * 128 + ar

        def g_(c):
            return 1024 + c * 128 + ar

        def pu(c):
            return 1280 + c * 128 + ar
        plist = [
            np.concatenate([qc(0), qr(0), qc(1), qr(1)]),
            np.concatenate([qc(2), qr(2), qc(3), qr(3)]),
            np.concatenate([kcol, krcol, vcol]),
            np.concatenate([a_(0), g_(0), a_(1), g_(1)]),
            np.concatenate([pu(0), pu(1)]),
        ]
        for pi, cols in enumerate(plist):
            m = _kmajor(w[:, cols], 8)
            out[l, pi, :, :m.shape[1]] = m
        for o in range(2):
            m = _kmajor(w_out[l][:, o * 512:(o + 1) * 512], 8)
            out[l, 5 + o, :, :m.shape[1]] = m
        for j in range(11):
            cols = np.concatenate([2 * j * 128 + np.arange(256), DFF + 2 * j * 128 + np.arange(256)])
            m = _kmajor(w_ffn_in[l][:, cols], 8)
            out[l, 7 + j, :, :m.shape[1]] = m
        for o in range(8):
            m = _kmajor(w_ffn_out[l][:, o * 128:(o + 1) * 128], KF)
            out[l, 18 + o, :, :m.shape[1]] = m
    return out


def _fm(v, nchunk):
    return np.ascontiguousarray(v.reshape(nchunk, 128).T)


def _host_small(b_mod, norm1_g, norm2_g, conv_dw, conv_dw_b, conv_ln_g, conv_ln_b, attn_sink,
                pool_w, pool_scale, final_g):
    sp = np.zeros((128, NSP), np.float32)
    for l in range(DEPTH):
        o = l * LSP
        sp[:, o:o + 8] = _fm(norm1_g[l], 8)
        sp[:, o + 8:o + 16] = _fm(norm2_g[l], 8)
        sp[:, o + 16:o + 64] = _fm(b_mod[l], 48)
        dw = conv_dw[l]
        for c in range(2):
            sp[:, o + 64 + c * 31:o + 64 + (c + 1) * 31] = dw[:, c * 128:(c + 1) * 128].T
        sp[:, o + 126:o + 128] = _fm(conv_dw_b[l], 2)
        sp[:, o + 128:o + 130] = _fm(conv_ln_g[l], 2)
        sp[:, o + 130:o + 132] = _fm(conv_ln_b[l], 2)
        sp[:, o + 132:o + 134] = _fm(pool_scale[l], 2)
        sp[:, o + 134:o + 142] = attn_sink[l][None, :]
        pw = pool_w[l]
        po = 2 * LSP + 8 + l * 256
        for c in range(2):
            blk = np.zeros((128, 128), np.float32)
            blk[0:64, 0:64] = pw[2 * c]
            blk[64:128, 64:128] = pw[2 * c + 1]
            sp[:, po + c * 128:po + (c + 1) * 128] = blk
    sp[:, 2 * LSP:2 * LSP + 8] = _fm(final_g, 8)
    return sp


def _host_consts():
    ct = np.zeros((128, NCT), np.float32)
    _, sign = _rope_perm()
    t = np.arange(S)
    row = (t // 64).astype(np.float32)
    col = (t % 64).astype(np.float32)
    inv = (np.float32(10000.0) ** (-np.arange(0, 32, 2, dtype=np.float32) / np.float32(32))).astype(np.float32)
    ar_ = row[:, None] * inv[None, :]
    ac_ = col[:, None] * inv[None, :]
    ang = np.concatenate([ar_, ar_, ac_, ac_], axis=-1).astype(np.float32)
    cos = np.cos(ang).astype(np.float32).T
    sin = (np.sin(ang).astype(np.float32) * sign[None, :]).T
    ct[:, 0:S] = np.concatenate([cos, cos], 0)
    ct[:, S:2 * S] = np.concatenate([sin, sin], 0)
    p = np.arange(128)[:, None]
    q = np.arange(128)[None, :]
    ct[:, 4096:4224] = (p >= q).astype(np.float32)
    ct[:, 4224:4352] = (p <= q).astype(np.float32)
    wins = np.array([[2, 8], [4, 16]])
    for half in range(2):
        for c in range(2):
            win = wins[half, c]
            ps = slice(half * 64, (half + 1) * 64)
            ct[ps, 4352 + c] = 1.0 / win
            for e in range(2):
                for i in range(8):
                    if e == 0:
                        tt = i
                        cnt = (tt + win - 1 - win // 2) - max(tt - win // 2, 0) + 1
                    else:
                        tt = -8 + i
                        cnt = min(tt + win - 1 - win // 2, -1) - (tt - win // 2) + 1
                    ct[ps, 4354 + c * 16 + e * 8 + i] = 1.0 / cnt
    ct[:, 4386] = D * EPS
    ct[:, 4387] = EPS
    ct[:, 4388:4516] = np.eye(128, dtype=np.float32)
    return ct


DEBUG = {"on": False, "taps": []}


def build_program():
    nc = bass.Bass("TRN2", target_bir_lowering=False)

    def tap(name, src, shape, dt, reads):
        if not DEBUG["on"] or any(t[0] == name for t in DEBUG["taps"]):
            return
        d = nc.dram_tensor("dbg_" + name, list(shape), dt, kind="ExternalOutput").ap()
        DEBUG["taps"].append((name, shape, dt))
        P.dma("pool", "dbg_" + name, (lambda e: e.dma_start(out=d, in_=src)), reads=reads)
        store_tokens.append(("ch:dbg_" + name, 1))
    _ALLOC["off"] = SB_BASE
    xT_d = nc.dram_tensor("xT", [NB, 128, 8 * S], F32, kind="ExternalInput").ap()
    cxT_d = nc.dram_tensor("cxT", [NB, 128, 8 * LC], F32, kind="ExternalInput").ap()
    cT_d = nc.dram_tensor("cT", [128, 40], F32, kind="ExternalInput").ap()
    wmod_d = nc.dram_tensor("wmod", [DEPTH, 8, 128, 8 * 768], F32, kind="ExternalInput").ap()
    sp_d = nc.dram_tensor("sp", [128, NSP], F32, kind="ExternalInput").ap()
    ct_d = nc.dram_tensor("ct", [128, NCT], F32, kind="ExternalInput").ap()
    wbig_d = nc.dram_tensor("wbig", [DEPTH, NPIECE, 128, PCOLS], F32, kind="ExternalInput").ap()
    y_d = nc.dram_tensor("yT", [NB, 128, 8 * S], F32, kind="ExternalOutput").ap()
    wb16_d = nc.dram_tensor("wb16", [DEPTH, NPIECE, 128, PCOLS], BF16, kind="Internal").ap()
    cs16_d = nc.dram_tensor("cs16", [128, 2 * S], BF16, kind="Internal").ap()

    P = Prog(nc)
    store_tokens = []
    G = Scope(P, nc)

    xT = G.sb("xT_s", [128, 8, S], F32)
    cxT = G.sb("cxT_s", [128, 8, LC], F32)
    x_res = [G.res("x%d" % i) for i in range(4)]
    cx_res = G.res("cx")
    csd_res = P.res("csd")
    masks = G.sb("masks", [128, 2, 512], BF16)
    ones = G.sb("ones", [128, 128], BF16)
    ident = G.sb("ident", [128, 128], BF16)
    spt = G.sb("spt", [128, 2 * LSP + 8], F32)
    ctsm = G.sb("ctsm", [128, 36], F32)
    pwb = G.sb("pwb", [128, 2, 256], BF16)
    const_res = G.res("const")
    modT = G.sb("modT", [128, 2, 48, 5], F32)
    A12 = G.sb("A12", [128, 2, 2, 5, 8], F32)
    fg32 = G.sb("fg32", [128, 8], F32)
    sexp = G.sb("sexp", [128, 2, 8], F32)
    cTb = G.sb("cTb", [128, 8, 5], BF16)
    cTb_res = G.res("cTb")
    mod_res = G.res("mod")
    ring = [G.sb("ring%d" % i, [128, PCOLS], BF16) for i in range(3)]
    ring_res = [G.res("ring%d" % i) for i in range(3)]
    NSCR = 6
    scr_t = [G.sb("scr%d" % i, [128, 544], F32) for i in range(NSCR)]
    scr_r = [G.res("scr%d" % i) for i in range(NSCR)]
    rsn_t = [G.sb("rsn%d" % i, [128, 512], F32) for i in range(1)]
    rsn_r = [G.res("rsn%d" % i) for i in range(1)]
    sq_t = [G.sb("sq%d" % i, [128, 512], BF16) for i in range(3)]
    sq_r = [G.res("sq%d" % i) for i in range(3)]
    banks = [nc.alloc_psum_tensor("bank%d" % i, [128, 512], F32) for i in range(8)]
    bank_r = [G.res("bank%d" % i) for i in range(8)]
    wscr_res = [[P.res("wscr") for _ in range(NPIECE)] for _ in range(DEPTH)]

    state = {"scr": 0, "sq": 0, "rsn": 0}
    free_banks = list(range(8))

    def scr():
        i = state["scr"]
        state["scr"] = (i + 1) % NSCR
        return scr_t[i], scr_r[i]

    def sqb():
        i = state["sq"]
        state["sq"] = (i + 1) % 3
        return sq_t[i], sq_r[i]

    def balloc():
        i = free_banks.pop(0)
        return i

    def bfree(i):
        free_banks.append(i)

    def sps(l, off, n=1):
        o = l * LSP + off
        return spt[:, o:o + n]

    P.dma("sp", "const1", lambda e: e.dma_start(out=spt[:], in_=sp_d[:, 0:2 * LSP + 8]), writes=[const_res])
    P.dma("sp", "const2", lambda e: e.dma_start(out=ctsm[:], in_=ct_d[:, 4352:4388]), writes=[const_res])

    def convert_weights(l, groups=("A", "B")):
        for grp, plist in (("A", range(0, 5)), ("B", range(5, NPIECE))):
            if grp not in groups:
                continue
            ch = "wcv%s%d" % (grp, l)
            for pi in plist:
                n = PIECE_N[pi]
                P.dma("pool", ch, (lambda e, l=l, pi=pi, n=n: e.dma_start(out=wb16_d[l, pi, :, 0:n], in_=wbig_d[l, pi, :, 0:n])),
                      writes=[wscr_res[l][pi]])
            for pi in plist:
                wscr_res[l][pi].w = ("ch:" + ch, len(plist))

    sched = []
    for b in range(NB):
        for l in range(DEPTH):
            for half in range(2):
                sched += [(l, pi) for pi in range(5)]
            sched += [(l, 5), (l, 6)]
            for grp in range(3):
                sched += [(l, 7 + j) for j in range(11)]
                sched += [(l, 18 + o) for o in range(8)]
    wstate = {"issued": 0, "used": 0}

    def w_issue_upto(k):
        while wstate["issued"] < min(k, len(sched)):
            i = wstate["issued"]
            l, pi = sched[i]
            n = PIECE_N[pi]
            s = i % 3
            P.dma("sp", "w%d" % s, (lambda e, l=l, pi=pi, n=n, s=s: e.dma_start(out=ring[s][:, 0:n], in_=wb16_d[l, pi, :, 0:n])),
                  reads=[wscr_res[l][pi]], writes=[ring_res[s]])
            wstate["issued"] += 1

    def w_next(l, pi, hold=0):
        i = wstate["used"]
        assert sched[i] == (l, pi), (i, sched[i], l, pi)
        w_issue_upto(i + 3 - hold)
        wstate["used"] += 1
        s = i % 3
        return ring[s], ring_res[s]

    def load_sample(b):
        P.dma("pool", "cxl", (lambda e, b=b: e.dma_start(out=cxT[:].rearrange("p a b -> p (a b)"), in_=cxT_d[b])), writes=[cx_res])
        for t in range(4):
            P.dma("pool", "xl%d" % t, (lambda e, b=b, t=t: e.dma_start(
                out=xT[:, :, t * 512:(t + 1) * 512],
                in_=xT_d[b].rearrange("p (c s) -> p c s", c=8)[:, :, t * 512:(t + 1) * 512])), writes=[x_res[t]])

    load_sample(0)

    PR = Scope(P, nc)
    ctb = PR.sb("ctb", [128, 4352], F32)
    ctb_res = PR.res("ctb")
    P.dma("sp", "const3", lambda e: e.dma_start(out=ctb[:], in_=ct_d[:, 0:4352]), writes=[ctb_res])
    cs0 = PR.sb("cs0", [128, 2, S], BF16)
    cs0_res = PR.res("cs0")
    P.op("act", lambda e: e.copy(out=cs0[:, 0, :], in_=ctb[:, 0:S]), reads=[ctb_res], writes=[cs0_res])
    P.op("dve", lambda e: e.tensor_copy(out=cs0[:, 1, :], in_=ctb[:, S:2 * S]), reads=[ctb_res], writes=[cs0_res])
    P.dma("sp", "const4", lambda e: e.dma_start(out=cs16_d[:, :], in_=cs0[:].rearrange("p a b -> p (a b)")), reads=[cs0_res], writes=[csd_res])
    for m in range(2):
        for r in range(4):
            P.op("dve", (lambda e, m=m, r=r: e.tensor_scalar(out=masks[:, m, r * 128:(r + 1) * 128], in0=ctb[:, 4096 + m * 128:4096 + (m + 1) * 128],
                                                              scalar1=30000.0, scalar2=-30000.0, op0=ALU.mult, op1=ALU.add)),
                 reads=[ctb_res], writes=[const_res])
    idf = PR.sb("idf", [128, 128], F32)
    idf_res = PR.res("idf")
    P.dma("sp", "const5", lambda e: e.dma_start(out=idf[:], in_=ct_d[:, 4388:4516]), writes=[idf_res])
    P.op("dve", lambda e: e.tensor_copy(out=ident[:], in_=idf[:]), reads=[idf_res], writes=[const_res])
    P.op("dve", lambda e: e.memset(ones[:], 1.0), writes=[const_res])
    pwf = PR.sb("pwf", [128, 512], F32)
    pwf_res = PR.res("pwf")
    P.dma("sp", "const6", lambda e: e.dma_start(out=pwf[:], in_=sp_d[:, 2 * LSP + 8:2 * LSP + 8 + 512]), writes=[pwf_res])
    for l in range(DEPTH):
        P.op("dve", (lambda e, l=l: e.tensor_copy(out=pwb[:, l, :], in_=pwf[:, l * 256:(l + 1) * 256])), reads=[pwf_res], writes=[const_res])
        P.op("act", (lambda e, l=l: e.activation(out=sexp[:, l, :], in_=sps(l, 134, 8), func=AF.Exp)), reads=[const_res], writes=[mod_res])
    P.op("dve", lambda e: e.tensor_scalar(out=fg32[:], in0=spt[:, 2 * LSP:2 * LSP + 8], scalar1=32.0, scalar2=None, op0=ALU.mult),
         reads=[const_res], writes=[mod_res])
    cT = PR.sb("cT_s", [128, 8, 5], F32)
    cT_res = PR.res("cT")
    P.dma("sp", "const7", lambda e: e.dma_start(out=cT[:].rearrange("p a b -> p (a b)"), in_=cT_d[:, :]), writes=[cT_res])
    P.op("act", lambda e: e.activation(out=cTb[:], in_=cT[:], func=AF.Silu), reads=[cT_res], writes=[cTb_res])

    def mod_layer(l, scope, nbuf, after_piece1=None):
        wm = [scope.sb("wm%d" % i, [128, 8, 768], BF16) for i in range(nbuf)]
        wm_res = [scope.res("wm%d" % i) for i in range(nbuf)]
        bk = balloc()
        bview = banks[bk][:, 0:240].rearrange("p (a b) -> p a b", b=5)
        for pi in range(8):
            s = pi % nbuf
            P.dma("pool", "wm%d_%d" % (l, s), (lambda e, pi=pi, s=s: e.dma_start(out=wm[s][:].rearrange("p a b -> p (a b)"), in_=wmod_d[l, pi, :, :])),
                  writes=[wm_res[s]])
            if pi == min(nbuf, 8) - 1 and after_piece1 is not None:
                after_piece1()
            for j in range(6):
                nchunk = pi * 6 + j
                for k in range(8):
                    P.op("pe", (lambda e, s=s, j=j, k=k, nchunk=nchunk: e.matmul(
                        bview[:, nchunk, :], wm[s][:, k, j * 128:(j + 1) * 128], cTb[:, k, :], start=(k == 0), stop=(k == 7))),
                        reads=[wm_res[s], cTb_res], writes=[bank_r[bk]])
        for j in range(5):
            P.op("dve", (lambda e, j=j: e.tensor_tensor(out=modT[:, l, :, j], in0=bview[:, :, j], in1=sps(l, 16, 48), op=ALU.add)),
                 reads=[bank_r[bk], const_res], writes=[mod_res])
        bfree(bk)
        for which in range(2):
            sc0 = 8 if which == 0 else 32
            for j in range(5):
                P.op("dve", (lambda e, which=which, j=j, sc0=sc0: e.tensor_scalar(
                    out=A12[:, l, which, j, :], in0=modT[:, l, sc0:sc0 + 8, j], scalar1=1.0, scalar2=32.0, op0=ALU.add, op1=ALU.mult)),
                    reads=[mod_res], writes=[mod_res])
                P.op("dve", (lambda e, which=which, j=j: e.tensor_tensor(
                    out=A12[:, l, which, j, :], in0=A12[:, l, which, j, :], in1=sps(l, 8 * which, 8), op=ALU.mult)),
                    reads=[mod_res, const_res], writes=[mod_res])

    mod_layer(0, PR, 4, after_piece1=lambda: convert_weights(0, ("A",)))
    convert_weights(0, ("B",))
    convert_weights(1)
    tap('modT', modT[:].rearrange('p a b c -> p (a b c)'), [128, 480], F32, [mod_res])
    tap('A12', A12[:].rearrange('p a b c d -> p (a b c d)'), [128, 160], F32, [mod_res])
    PR.close()

    def norm(src, TT, a_ap, sh_ap, dst, reads, writes):
        sqs, mms, tail = norm_split(src, TT, a_ap, sh_ap, dst, reads, writes)
        for c in range(8):
            sqs[c]()
            mms[c]()
        tail()

    def norm_split(src, TT, a_ap, sh_ap, dst, reads, writes):
        st = {}

        def sq_item(c):
            st[c] = sqb()
            sqt, sqr = st[c]
            P.op("act", (lambda e: e.activation(out=sqt[:, 0:TT], in_=src(c), func=AF.Square)), reads=reads, writes=[sqr])

        def mm_item(c):
            if "bk" not in st:
                st["bk"] = balloc()
            bk = st["bk"]
            sqt, sqr = st[c]
            P.op("pe", (lambda e: e.matmul(banks[bk][:, 0:TT], ones[:, :], sqt[:, 0:TT], start=(c == 0), stop=(c == 7))),
                 reads=[sqr, const_res], writes=[bank_r[bk]])

        def tail():
            norm_tail(st["bk"], src, TT, a_ap, sh_ap, dst, reads, writes)
        return [(lambda c=c: sq_item(c)) for c in range(8)], [(lambda c=c: mm_item(c)) for c in range(8)], tail

    def norm_tail(bk, src, TT, a_ap, sh_ap, dst, reads, writes):
        ri = 0
        rs, rsr = rsn_t[ri], rsn_r[ri]
        P.op("act", lambda e: e.activation(out=rs[:, 0:TT], in_=banks[bk][:, 0:TT], func=AF.Ln, bias=ctsm[:, 34:35], scale=1.0),
             reads=[bank_r[bk], const_res], writes=[rsr])
        P.op("act", lambda e: e.activation(out=rs[:, 0:TT], in_=rs[:, 0:TT], func=AF.Exp, scale=-0.5), reads=[rsr], writes=[rsr])
        bfree(bk)
        for c in range(8):
            if sh_ap is not None:
                tt, ttr = scr()
                P.op("dve", (lambda e, c=c, tt=tt: e.scalar_tensor_tensor(out=tt[:, 0:TT], in0=src(c), scalar=a_ap(c), in1=rs[:, 0:TT],
                                                                          op0=ALU.mult, op1=ALU.mult)),
                     reads=list(reads) + [rsr, mod_res], writes=[ttr])
                P.op("act", (lambda e, c=c, tt=tt: e.activation(out=dst(c), in_=tt[:, 0:TT], func=AF.Identity, bias=sh_ap(c), scale=1.0)),
                     reads=[ttr, mod_res], writes=writes)
            else:
                P.op("dve", (lambda e, c=c: e.scalar_tensor_tensor(out=dst(c), in0=src(c), scalar=a_ap(c), in1=rs[:, 0:TT],
                                                                   op0=ALU.mult, op1=ALU.mult)),
                     reads=list(reads) + [rsr, mod_res], writes=writes)

    def mm_acc(bk, ncol, lhs_fn, rhs_fn, nk, reads):
        for k in range(nk):
            P.op("pe", (lambda e, k=k: e.matmul(banks[bk][:, 0:ncol], lhs_fn(k), rhs_fn(k), start=(k == 0), stop=(k == nk - 1))),
                 reads=reads, writes=[bank_r[bk]])


    def run_sample(b):
        if b > 0:
            load_sample(b)

        def run_layer(l):
            last = (l == DEPTH - 1)
            SQ = Scope(P, nc)
            qT = SQ.sb("qT", [128, 4, S], BF16)
            q_res = [SQ.res("q%d" % i) for i in range(16)]
            kTs = SQ.sb("kTs", [128, 2, S], BF16)
            k_res = [SQ.res("k%d" % i) for i in range(4)]
            Va = SQ.sb("Va", [128, 16, 2, 128], BF16)
            v_res = [SQ.res("v%d" % i) for i in range(4)]
            hb = SQ.sb("hb", [128, 2, S + 30], BF16)
            h_res = [SQ.res("h%d" % i) for i in range(4)]
            pub = SQ.sb("pub", [128, 2, S + 16], BF16)
            pu_res = [SQ.res("pu%d" % i) for i in range(4)]
            qTc = SQ.sb("qTc", [128, 4, LC], BF16)
            qc_res = [SQ.res("qc%d" % i) for i in range(2)]
            kTc = SQ.sb("kTc", [128, 2, LC], BF16)
            kc_res = SQ.res("kc")
            Vc = SQ.sb("Vc", [128, 2, 2, 128], BF16)
            vc_res = SQ.res("vc")
            hc = SQ.sb("hc", [128, 2, LC + 30], BF16)
            hc_res = SQ.res("hc")
            puc = SQ.sb("puc", [128, 2, LC + 16], BF16)
            puc_res = SQ.res("puc")
            P.op("dve", lambda e: e.memset(hb[:, :, 0:15], 0.0), writes=[h_res[0]])
            P.op("dve", lambda e: e.memset(hb[:, :, S + 15:S + 30], 0.0), writes=[h_res[3]])
            P.op("dve", lambda e: e.memset(pub[:, :, 0:8], 0.0), writes=[pu_res[0]])
            P.op("dve", lambda e: e.memset(pub[:, :, S + 8:S + 16], 0.0), writes=[pu_res[3]])
            P.op("dve", lambda e: e.memset(Va[:, :, :, 64:128], 1.0), writes=v_res)
            P.op("dve", lambda e: e.memset(kTs[64:128, :, :], 0.0), writes=k_res)
            P.op("dve", lambda e: e.memset(kTc[64:128, :, :], 0.0), writes=[kc_res])
            P.op("dve", lambda e: e.memset(Vc[:, :, :, 64:128], 1.0), writes=[vc_res])
            if not last:
                P.op("dve", lambda e: e.memset(hc[:, :, 0:15], 0.0), writes=[hc_res])
                P.op("dve", lambda e: e.memset(hc[:, :, LC + 15:LC + 30], 0.0), writes=[hc_res])
                P.op("dve", lambda e: e.memset(puc[:, :, 0:8], 0.0), writes=[puc_res])
                P.op("dve", lambda e: e.memset(puc[:, :, LC + 8:LC + 16], 0.0), writes=[puc_res])

            def modv(off, j):
                return lambda c: modT[:, l, off + c, j:j + 1]

            PA = Scope(P, nc)
            xm = [PA.sb("xm%d" % i, [128, 8, 512], BF16) for i in range(2)]
            xm_res = [PA.res("xm%d" % i) for i in range(2)]
            xmc = PA.sb("xmc", [128, 8, LC], BF16)
            xmc_res = PA.res("xmc")
            cs = PA.sb("cs", [128, 2, S], BF16)
            cs_res = PA.res("cs")
            P.dma("sp", "csl", lambda e: e.dma_start(out=cs[:].rearrange("p a b -> p (a b)"), in_=cs16_d[:, :]), reads=[csd_res], writes=[cs_res])

            def proj_tile(pi, slot, slot_r, xmt, xmr, TT, t0, is_ctx, tix):
                ncols = PIECE_N[pi] // 8
                sv = slot[:, 0:PIECE_N[pi]].rearrange("p (k n) -> p k n", k=8)

                def acc(col0, width=128):
                    bk = balloc()
                    mm_acc(bk, TT, lambda k: sv[:, k, col0:col0 + width], lambda k: xmt[:, k, 0:TT], 8, [slot_r, xmr])
                    return bk
                if pi in (0, 1):
                    for i in range(2):
                        c = 2 * pi + i
                        bq = acc(i * 256)
                        if is_ctx:
                            P.op("act", (lambda e, c=c, bq=bq: e.copy(out=qTc[:, c, :], in_=banks[bq][:, 0:TT])),
                                 reads=[bank_r[bq]], writes=qc_res)
                            bfree(bq)
                            continue
                        br = acc(i * 256 + 128)
                        t1, t1r = scr()
                        t2, t2r = scr()
                        P.op("dve", (lambda e, bq=bq, t1=t1: e.tensor_tensor(out=t1[:, 0:TT], in0=banks[bq][:, 0:TT], in1=cs[:, 0, t0:t0 + TT], op=ALU.mult)),
                             reads=[bank_r[bq], cs_res], writes=[t1r])
                        P.op("dve", (lambda e, br=br, t2=t2: e.tensor_tensor(out=t2[:, 0:TT], in0=banks[br][:, 0:TT], in1=cs[:, 1, t0:t0 + TT], op=ALU.mult)),
                             reads=[bank_r[br], cs_res], writes=[t2r])
                        bfree(bq)
                        bfree(br)
                        P.op("dve", (lambda e, c=c, t1=t1, t2=t2: e.tensor_tensor(out=qT[:, c, t0:t0 + TT], in0=t1[:, 0:TT], in1=t2[:, 0:TT], op=ALU.add)),
                             reads=[t1r, t2r], writes=q_res[4 * tix:4 * tix + 4])
                elif pi == 2:
                    bq = acc(0)
                    kt, ktr = sqb()
                    if is_ctx:
                        P.op("act", (lambda e, bq=bq, kt=kt: e.copy(out=kt[:, 0:TT], in_=banks[bq][:, 0:TT])), reads=[bank_r[bq]], writes=[ktr])
                        bfree(bq)
                    else:
                        br = acc(128)
                        t1, t1r = scr()
                        t2, t2r = scr()
                        P.op("dve", (lambda e, bq=bq, t1=t1: e.tensor_tensor(out=t1[:, 0:TT], in0=banks[bq][:, 0:TT], in1=cs[:, 0, t0:t0 + TT], op=ALU.mult)),
                             reads=[bank_r[bq], cs_res], writes=[t1r])
                        P.op("dve", (lambda e, br=br, t2=t2: e.tensor_tensor(out=t2[:, 0:TT], in0=banks[br][:, 0:TT], in1=cs[:, 1, t0:t0 + TT], op=ALU.mult)),
                             reads=[bank_r[br], cs_res], writes=[t2r])
                        bfree(bq)
                        bfree(br)
                        P.op("dve", (lambda e, kt=kt, t1=t1, t2=t2: e.tensor_tensor(out=kt[:, 0:TT], in0=t1[:, 0:TT], in1=t2[:, 0:TT], op=ALU.add)),
                             reads=[t1r, t2r], writes=[ktr])
                    kdst = kTc if is_ctx else kTs
                    kw = [kc_res] if is_ctx else [k_res[tix]]
                    for g in range(2):
                        P.op("act", (lambda e, g=g, kt=kt, kdst=kdst: e.copy(out=kdst[0:64, g, t0:t0 + TT], in_=kt[g * 64:(g + 1) * 64, 0:TT])),
                             reads=[ktr], writes=kw)
                    nblk = TT // 128
                    bv = balloc()
                    for blk in range(nblk):
                        for k in range(8):
                            P.op("pe", (lambda e, blk=blk, k=k: e.matmul(banks[bv][:, blk * 128:(blk + 1) * 128], xmt[:, k, blk * 128:(blk + 1) * 128],
                                                                         sv[:, k, 256:384], start=(k == 0), stop=(k == 7))),
                                 reads=[slot_r, xmr], writes=[bank_r[bv]])
                    vdst = Vc if is_ctx else Va
                    kb0 = t0 // 128
                    P.op("act", (lambda e, bv=bv, vdst=vdst: e.copy(
                        out=vdst[:, kb0:kb0 + nblk, :, 0:64],
                        in_=banks[bv][:, 0:TT].rearrange("p (b g d) -> p b g d", g=2, d=64))),
                        reads=[bank_r[bv]], writes=([vc_res] if is_ctx else [v_res[tix]]))
                    bfree(bv)
                elif pi == 3:
                    hdst = hc if is_ctx else hb
                    hw = [hc_res] if is_ctx else [h_res[tix]]
                    for c in range(2):
                        ba = acc(c * 256)
                        bg = acc(c * 256 + 128)
                        sg, sgr = scr()
                        P.op("act", (lambda e, bg=bg, sg=sg: e.activation(out=sg[:, 0:TT], in_=banks[bg][:, 0:TT], func=AF.Sigmoid)),
                             reads=[bank_r[bg]], writes=[sgr])
                        bfree(bg)
                        P.op("dve", (lambda e, c=c, ba=ba, sg=sg, hdst=hdst: e.tensor_tensor(out=hdst[:, c, 15 + t0:15 + t0 + TT], in0=banks[ba][:, 0:TT],
                                                                                      in1=sg[:, 0:TT], op=ALU.mult)),
                             reads=[bank_r[ba], sgr], writes=hw)
                        bfree(ba)
                elif pi == 4:
                    pdst = puc if is_ctx else pub
                    pw_ = [puc_res] if is_ctx else [pu_res[tix]]
                    for c in range(2):
                        bp = acc(c * 128)
                        P.op("act", (lambda e, c=c, bp=bp, pdst=pdst: e.copy(out=pdst[:, c, 8 + t0:8 + t0 + TT], in_=banks[bp][:, 0:TT])),
                             reads=[bank_r[bp]], writes=pw_)
                        bfree(bp)

            norm(lambda c: cxT[:, c, :], LC, lambda c: A12[:, l, 0, 4, c:c + 1], modv(0, 4),
                 lambda c: xmc[:, c, :], [cx_res], [xmc_res])
            for half in range(2):
                for i in range(2):
                    t = 2 * half + i
                    norm((lambda c, t=t: xT[:, c, t * 512:(t + 1) * 512]), 512, lambda c: A12[:, l, 0, b, c:c + 1], modv(0, b),
                         (lambda c, i=i: xm[i][:, c, :]), [x_res[t]], [xm_res[i]])
                for pi in range(5):
                    slot, slot_r = w_next(l, pi)
                    if half == 0 and ((not last) or pi == 2):
                        proj_tile(pi, slot, slot_r, xmc, xmc_res, LC, 0, True, 0)
                    for i in range(2):
                        t = 2 * half + i
                        proj_tile(pi, slot, slot_r, xm[i], xm_res[i], 512, t * 512, False, t)
            tap('xm1', xm[1][:].rearrange('p a b -> p (a b)'), [128, 4096], BF16, xm_res)
            tap('qT', qT[:].rearrange('p a b -> p (a b)'), [128, 4 * S], BF16, q_res)
            tap('kTs', kTs[0:64].rearrange('p a b -> p (a b)'), [64, 2 * S], BF16, k_res)
            tap('Va', Va[:].rearrange('p a b c -> p (a b c)'), [128, 4096], BF16, v_res)
            tap('hb', hb[:].rearrange('p a b -> p (a b)'), [128, 2 * (S + 30)], BF16, h_res)
            tap('pub', pub[:].rearrange('p a b -> p (a b)'), [128, 2 * (S + 16)], BF16, pu_res)
            tap('kTc', kTc[0:64].rearrange('p a b -> p (a b)'), [64, 2 * LC], BF16, [kc_res])
            tap('qTc', qTc[:].rearrange('p a b -> p (a b)'), [128, 4 * LC], BF16, qc_res)
            PA.close()

            PB = Scope(P, nc)
            PT = [PB.sb("PT%d" % i, [128, 512], BF16) for i in range(4)]
            PT_r = [PB.res("PT%d" % i) for i in range(4)]
            ptst = {"i": 0}
            qg = [PB.sb("qg%d" % i, [128, 8, 128], BF16) for i in range(2)]
            qodd_r = [PB.res("qg%d" % i) for i in range(2)]
            for i_ in range(2):
                P.op("dve", (lambda e, i_=i_: e.memset(qg[i_][64:128, :, :], 0.0)), writes=[qodd_r[i_]])
            otmp = PB.sb("otmp", [64, 2, 128], BF16)
            otmp_r = PB.res("otmp")
            mixc = [PB.sb("mixc%d" % i, [128, 4, 512], BF16) for i in range(2)]
            mixc_r = [PB.res("mixc%d" % i) for i in range(2)]
            dg = PB.sb("dg", [128, 62, 128], BF16)
            dg_r = PB.res("dg")
            def build_dg(lo, hi):
                for ci in range(lo, hi):
                    if True:
                        P.op("dve", (lambda e, ci=ci: e.tensor_scalar(out=dg[:, ci, :], in0=ident[:, :], scalar1=sps(l, 64 + ci), scalar2=None, op0=ALU.mult)),
                             reads=[const_res], writes=[dg_r])
                    else:
                        P.op("act", (lambda e, ci=ci: e.activation(out=dg[:, ci, :], in_=ident[:, :], func=AF.Identity, scale=sps(l, 64 + ci))),
                             reads=[const_res], writes=[dg_r])

            def ctx_kb(j):
                return (lambda g: kTc[:, g, j * 128:(j + 1) * 128], lambda g: Vc[:, j, g, :], None, [kc_res, vc_res])

            def loc_kb(j, m):
                return (lambda g: kTs[:, g, j * 128:(j + 1) * 128], lambda g: Va[:, j, g, :], m, [k_res[j // 4], v_res[j // 4]])

            qblocks = []
            if not last:
                for qb in range(2):
                    qblocks.append((qTc, qc_res[qb], qb, [ctx_kb(0), ctx_kb(1)]))
            for qb in range(16):
                kbs = [ctx_kb(0), ctx_kb(1)]
                if qb > 0:
                    kbs.append(loc_kb(qb - 1, 0))
                kbs.append(loc_kb(qb, None))
                if qb < 15:
                    kbs.append(loc_kb(qb + 1, 1))
                qblocks.append((qT, q_res[qb], qb, kbs))

            def emit_qodd(n):
                qbuf, qr, qb, kbs = qblocks[n]
                q0 = qb * 128
                qg4 = qg[n % 2][0:64, :, :].rearrange("p (c par) q -> p c par q", par=2)
                P.op("dve", lambda e: e.tensor_copy(out=qg4[:, :, 0, :], in_=qbuf[0:64, :, q0:q0 + 128]), reads=[qr], writes=[qodd_r[n % 2]])
                P.op("dve", lambda e: e.tensor_copy(out=qg4[:, :, 1, :], in_=qbuf[64:128, :, q0:q0 + 128]), reads=[qr], writes=[qodd_r[n % 2]])

            steps = []
            for n, (qbuf, qr, qb, kbs) in enumerate(qblocks):
                for g in range(2):
                    for ki, kb in enumerate(kbs):
                        steps.append((n, g, kb, ki == 0, ki == len(kbs) - 1, g == 0 and ki == 0))
            obank = {}
            deferred = []

            def do_s(n, g, kb):
                qbuf, qr, qb, kbs = qblocks[n]
                q0 = qb * 128
                bs = balloc()
                masked = kb[2] is not None
                if masked:
                    P.op("pe", (lambda e, bs=bs, m=kb[2]: e.matmul(banks[bs][:, :], ident[:, :], masks[:, m, :], start=True, stop=False, skip_group_check=True)),
                         reads=[const_res], writes=[bank_r[bs]])
                rhs = qg[n % 2][:, 4 * g:4 * g + 4, :].rearrange("p h q -> p (h q)")
                P.op("pe", (lambda e, rhs=rhs, bs=bs, kb=kb, g=g, masked=masked: e.matmul(
                    banks[bs][:, :], kb[0](g), rhs, start=(not masked), stop=True, skip_group_check=True)),
                    reads=[qodd_r[n % 2]] + kb[3], writes=[bank_r[bs]])
                pti = ptst["i"]
                ptst["i"] = (pti + 1) % 4
                P.op("act", (lambda e, bs=bs, pti=pti: e.activation(out=PT[pti][:, :], in_=banks[bs][:, :], func=AF.Exp, scale=0.125)),
                     reads=[bank_r[bs]], writes=[PT_r[pti]])
                bfree(bs)
                return pti

            def do_pv(n, g, kb, pti, first, lastkb):
                if first:
                    obank[(n, g)] = balloc()
                ob = obank[(n, g)]
                P.op("pe", (lambda e: e.matmul(banks[ob][:, :], kb[1](g), PT[pti][:, :], start=first, stop=lastkb, skip_group_check=True)),
                     reads=[PT_r[pti]] + kb[3], writes=[bank_r[ob]])

            def finalize(n, g):
                qbuf, qr, qb, kbs = qblocks[n]
                q0 = qb * 128
                ob = obank[(n, g)]
                ssb, ssr = rsn_t[0], rsn_r[0]
                for hl in range(4):
                    h = 4 * g + hl
                    P.op("dve", (lambda e, hl=hl, h=h: e.tensor_scalar(out=ssb[0:64, hl * 128:(hl + 1) * 128], in0=banks[ob][64:128, hl * 128:(hl + 1) * 128],
                                                                       scalar1=sexp[64:128, l, h:h + 1], scalar2=None, op0=ALU.add)),
                         reads=[bank_r[ob], mod_res], writes=[ssr])
                P.op("act", lambda e: e.activation(out=ssb[0:64, 0:512], in_=ssb[0:64, 0:512], func=AF.Ln), reads=[ssr], writes=[ssr])
                P.op("act", lambda e: e.activation(out=ssb[0:64, 0:512], in_=ssb[0:64, 0:512], func=AF.Exp, scale=-1.0), reads=[ssr], writes=[ssr])
                ob4 = banks[ob][0:64, :].rearrange("p (a two q) -> p a two q", a=2, two=2)
                ss4 = ssb[0:64, 0:512].rearrange("p (a two q) -> p a two q", a=2, two=2)
                P.op("dve", lambda e: e.tensor_tensor(out=qbuf[0:64, 2 * g:2 * g + 2, q0:q0 + 128], in0=ob4[:, :, 0, :], in1=ss4[:, :, 0, :], op=ALU.mult),
                     reads=[bank_r[ob], ssr], writes=[qr])
                P.op("dve", lambda e: e.tensor_tensor(out=qbuf[64:128, 2 * g:2 * g + 2, q0:q0 + 128], in0=ob4[:, :, 1, :], in1=ss4[:, :, 1, :], op=ALU.mult),
                     reads=[bank_r[ob], ssr], writes=[qr])
                bfree(ob)

            def conv_mm(hbuf, hreads, TT, t0):
                cbanks = []
                for c in range(2):
                    bc = balloc()
                    cbanks.append(bc)
                    for tap in range(31):
                        P.op("pe", (lambda e, c=c, tap=tap, bc=bc: e.matmul(banks[bc][:, 0:TT], dg[:, c * 31 + tap, :], hbuf[:, c, t0 + tap:t0 + tap + TT],
                                                                          start=(tap == 0), stop=(tap == 30))),
                             reads=hreads + [dg_r], writes=[bank_r[bc]])
                return cbanks

            def pool_stages(pbuf, preads, n, TT, t0, mix, mix_r):
                W = TT + 16
                st = {}

                def chain(c):
                    X = lambda lo, hi: pbuf[:, c, t0 + lo:t0 + hi]
                    T1, T1r = scr()
                    P.op("dve", (lambda e: e.tensor_tensor(out=T1[:, 1:W - 1], in0=X(0, W - 2), in1=X(1, W - 1), op=ALU.add)),
                         reads=preads, writes=[T1r])
                    yv, yvr = sqb()
                    st[c] = (yv, yvr)
                    if c == 0:
                        T2, T2r = scr()
                        P.op("dve", (lambda e: e.tensor_tensor(out=T2[64:128, 0:TT], in0=T1[64:128, 7:7 + TT], in1=T1[64:128, 9:9 + TT], op=ALU.add)),
                             reads=[T1r], writes=[T2r])
                        srcs = [(T1, T1r, 8, 0), (T2, T2r, 0, 64)]
                    else:
                        T2, T2r = scr()
                        T3, T3r = scr()
                        P.op("dve", (lambda e: e.tensor_tensor(out=T2[:, 3:W - 1], in0=T1[:, 1:W - 3], in1=T1[:, 3:W - 1], op=ALU.add)),
                             reads=[T1r], writes=[T2r])
                        P.op("dve", (lambda e: e.tensor_tensor(out=T3[:, 7:W - 1], in0=T2[:, 3:W - 5], in1=T2[:, 7:W - 1], op=ALU.add)),
                             reads=[T2r], writes=[T3r])
                        P.op("dve", (lambda e: e.tensor_tensor(out=T1[64:128, 0:TT], in0=T3[64:128, 7:7 + TT], in1=T3[64:128, 15:15 + TT], op=ALU.add)),
                             reads=[T3r, T1r], writes=[T1r])
                        srcs = [(T3, T3r, 11, 0), (T1, T1r, 0, 64)]
                    for (Wt, Wr, off, p0) in srcs:
                        ps_ = slice(p0, p0 + 64)
                        P.op("dve", (lambda e, Wt=Wt, off=off, ps_=ps_: e.scalar_tensor_tensor(
                            out=yv[ps_, 0:TT], in0=Wt[ps_, off:off + TT], scalar=ctsm[ps_, c:c + 1], in1=pbuf[ps_, c, t0 + 8:t0 + 8 + TT],
                            op0=ALU.mult, op1=ALU.subtract)), reads=[Wr, const_res] + preads, writes=[yvr])
                        edges = []
                        if t0 == 0:
                            edges.append((0, 0))
                        if t0 + TT == n:
                            edges.append((1, TT - 8))
                        for (ei, eo) in edges:
                            et, etr = scr()
                            P.op("dve", (lambda e, Wt=Wt, off=off, ps_=ps_, et=et, ei=ei, eo=eo: e.tensor_tensor(
                                out=et[ps_, 0:8], in0=Wt[ps_, off + eo:off + eo + 8], in1=ctsm[ps_, 2 + c * 16 + ei * 8:2 + c * 16 + ei * 8 + 8], op=ALU.mult)),
                                reads=[Wr, const_res], writes=[etr])
                            P.op("dve", (lambda e, ps_=ps_, et=et, eo=eo: e.tensor_tensor(
                                out=yv[ps_, eo:eo + 8], in0=et[ps_, 0:8], in1=pbuf[ps_, c, t0 + 8 + eo:t0 + 8 + eo + 8], op=ALU.subtract)),
                                reads=[etr, yvr] + preads, writes=[yvr])

                def mmev(c):
                    yv, yvr = st[c]
                    bp = balloc()
                    P.op("pe", (lambda e: e.matmul(banks[bp][:, 0:TT], pwb[:, l, c * 128:(c + 1) * 128], yv[:, 0:TT], start=True, stop=True)),
                         reads=[yvr, const_res], writes=[bank_r[bp]])
                    P.op("dve", (lambda e: e.tensor_scalar(out=mix[:, 2 + c, 0:TT], in0=banks[bp][:, 0:TT], scalar1=sps(l, 132 + c), scalar2=None, op0=ALU.mult)),
                         reads=[bank_r[bp], const_res], writes=[mix_r])
                    bfree(bp)
                return [(0, lambda: chain(0)), (1, lambda: mmev(0)), (0, lambda: chain(1)), (1, lambda: mmev(1))]

            def ln_stages(get_cbanks, TT, mix, mix_r):
                st = {}

                def L1():
                    cbanks = get_cbanks()
                    accs = []
                    for c in range(2):
                        bc = cbanks[c]
                        acc, accr = scr()
                        accs.append((acc, accr))
                        P.op("dve", (lambda e, c=c, acc=acc, bc=bc: e.tensor_scalar(out=acc[:, 0:TT], in0=banks[bc][:, 0:TT], scalar1=sps(l, 126 + c),
                                                                                   scalar2=None, op0=ALU.add)),
                             reads=[bank_r[bc], const_res], writes=[accr])
                        bfree(bc)
                    st["accs"] = accs

                def L2():
                    b1 = balloc()
                    b2 = balloc()
                    st["b"] = (b1, b2)
                    for c in range(2):
                        acc, accr = st["accs"][c]
                        s1, s1r = sqb()
                        s2, s2r = sqb()
                        P.op("dve", (lambda e, acc=acc, s1=s1: e.tensor_copy(out=s1[:, 0:TT], in_=acc[:, 0:TT])), reads=[accr], writes=[s1r])
                        P.op("act", (lambda e, acc=acc, s2=s2: e.activation(out=s2[:, 0:TT], in_=acc[:, 0:TT], func=AF.Square)), reads=[accr], writes=[s2r])
                        P.op("pe", (lambda e, c=c, s1=s1: e.matmul(banks[b1][:, 0:TT], ones[:, :], s1[:, 0:TT], start=(c == 0), stop=(c == 1))),
                             reads=[s1r, const_res], writes=[bank_r[b1]])
                        P.op("pe", (lambda e, c=c, s2=s2: e.matmul(banks[b2][:, 0:TT], ones[:, :], s2[:, 0:TT], start=(c == 0), stop=(c == 1))),
                             reads=[s2r, const_res], writes=[bank_r[b2]])

                def L3():
                    b1, b2 = st["b"]
                    mu, mur = scr()
                    rs, rsr = scr()
                    st["mu"] = (mu, mur)
                    st["rs"] = (rs, rsr)
                    P.op("dve", lambda e: e.tensor_scalar(out=mu[:, 0:TT], in0=banks[b1][:, 0:TT], scalar1=1.0 / 256, scalar2=None, op0=ALU.mult),
                         reads=[bank_r[b1]], writes=[mur])
                    P.op("dve", lambda e: e.tensor_tensor(out=rs[:, 0:TT], in0=mu[:, 0:TT], in1=mu[:, 0:TT], op=ALU.mult), reads=[mur], writes=[rsr])
                    P.op("dve", lambda e: e.scalar_tensor_tensor(out=rs[:, 0:TT], in0=banks[b2][:, 0:TT], scalar=1.0 / 256, in1=rs[:, 0:TT],
                                                                 op0=ALU.mult, op1=ALU.subtract), reads=[bank_r[b2], rsr], writes=[rsr])
                    bfree(b1)
                    bfree(b2)

                def L4():
                    rs, rsr = st["rs"]
                    P.op("act", lambda e: e.activation(out=rs[:, 0:TT], in_=rs[:, 0:TT], func=AF.Ln, bias=ctsm[:, 35:36], scale=1.0),
                         reads=[rsr, const_res], writes=[rsr])
                    P.op("act", lambda e: e.activation(out=rs[:, 0:TT], in_=rs[:, 0:TT], func=AF.Exp, scale=-0.5), reads=[rsr], writes=[rsr])

                def L5():
                    mu, mur = st["mu"]
                    rs, rsr = st["rs"]
                    for c in range(2):
                        acc, accr = st["accs"][c]
                        P.op("dve", (lambda e, acc=acc: e.tensor_tensor(out=acc[:, 0:TT], in0=acc[:, 0:TT], in1=mu[:, 0:TT], op=ALU.subtract)),
                             reads=[accr, mur], writes=[accr])
                        P.op("dve", (lambda e, acc=acc: e.tensor_tensor(out=acc[:, 0:TT], in0=acc[:, 0:TT], in1=rs[:, 0:TT], op=ALU.mult)),
                             reads=[accr, rsr], writes=[accr])

                def L6():
                    for c in range(2):
                        acc, accr = st["accs"][c]
                        P.op("act", (lambda e, c=c, acc=acc: e.activation(out=mix[:, c, 0:TT], in_=acc[:, 0:TT], func=AF.Silu, bias=sps(l, 130 + c),
                                                                          scale=sps(l, 128 + c))), reads=[accr, const_res], writes=[mix_r])
                return [(0, L1), (2, L2), (0, L3), (0, L4), (0, L5), (0, L6)]

            def wout_tile(slots, TT, abuf, areads, t0, mix, mix_r, xdst, xres, j, ocs=range(8)):
                for oc in ocs:
                    slot, slot_r = slots[oc // 4]
                    sv = slot[:, :].rearrange("p (k n) -> p k n", k=8)
                    col0 = (oc % 4) * 128
                    bk = balloc()
                    for k in range(8):
                        rhs = abuf[:, k, t0:t0 + TT] if k < 4 else mix[:, k - 4, 0:TT]
                        P.op("pe", (lambda e, k=k, rhs=rhs, sv=sv, col0=col0, bk=bk: e.matmul(banks[bk][:, 0:TT], sv[:, k, col0:col0 + 128], rhs,
                                                                                       start=(k == 0), stop=(k == 7))),
                             reads=[slot_r, mix_r] + areads, writes=[bank_r[bk]])
                    P.op("dve", (lambda e, oc=oc, bk=bk: e.scalar_tensor_tensor(out=xdst(oc), in0=banks[bk][:, 0:TT], scalar=modT[:, l, 16 + oc, j:j + 1],
                                                                               in1=xdst(oc), op0=ALU.mult, op1=ALU.add)),
                         reads=[bank_r[bk], mod_res], writes=xres)
                    bfree(bk)

            tap('attn', qT[:].rearrange('p a b -> p (a b)'), [128, 4 * S], BF16, q_res)
            tap('attnc', qTc[:].rearrange('p a b -> p (a b)'), [128, 4 * LC], BF16, qc_res)
            wslots = [w_next(l, 5), w_next(l, 6, hold=1)]
            tiles = []
            if not last:
                tiles.append(dict(hbuf=hc, hreads=[hc_res], pbuf=puc, preads=[puc_res], n=LC, TT=LC, t0=0, abuf=qTc, areads=qc_res,
                                  xdst=(lambda oc: cxT[:, oc, :]), xres=[cx_res], j=4))
            for t in range(4):
                nb_h = [h_res[i] for i in range(max(0, t - 1), min(4, t + 2))]
                nb_p = [pu_res[i] for i in range(max(0, t - 1), min(4, t + 2))]
                tiles.append(dict(hbuf=hb, hreads=nb_h, pbuf=pub, preads=nb_p, n=S, TT=512, t0=t * 512, abuf=qT, areads=q_res[4 * t:4 * t + 4],
                                  xdst=(lambda oc, t=t: xT[:, oc, t * 512:(t + 1) * 512]), xres=[x_res[t]], j=b))
            from collections import deque
            work = deque()
            cbs = {}

            def conv_items(i):
                T = tiles[i]
                mx, mxr = mixc[i % 2], mixc_r[i % 2]

                def f_cm():
                    cbs[i] = conv_mm(T["hbuf"], T["hreads"], T["TT"], T["t0"])
                ps = pool_stages(T["pbuf"], T["preads"], T["n"], T["TT"], T["t0"], mx, mxr)
                ls = ln_stages((lambda: cbs[i]), T["TT"], mx, mxr)
                return [(2, f_cm), ps[0], ps[2], ps[1], ps[3]] + ls

            def wout_items(i):
                T = tiles[i]
                return [(1, (lambda oc=oc: wout_tile(wslots, T["TT"], T["abuf"], T["areads"], T["t0"], mixc[i % 2], mixc_r[i % 2],
                                                     T["xdst"], T["xres"], T["j"], ocs=[oc]))) for oc in range(8)]

            def enq_conv(i, with_wout=None):
                ci = conv_items(i)
                wi = wout_items(with_wout) if with_wout is not None else []
                order = [0, 'w', 1, 'w', 2, 'w', 3, 'w', 4, 5, 'w', 6, 'w', 7, 8, 'w', 9, 'w', 10]
                if with_wout is None:
                    order = [1, 2, 0, 3, 4, 5, 6, 7, 8, 9, 10]
                for o in order:
                    if o == 'w':
                        if wi:
                            work.append(wi.pop(0))
                    else:
                        work.append(ci[o])
                for it in wi:
                    work.append(it)

            def enq_wout(i):
                for it in wout_items(i):
                    work.append(it)

            def pump(reserve):
                if work and len(free_banks) >= work[0][0] + reserve:
                    work.popleft()[1]()
                    return True
                return False

            ntl = len(tiles)
            qb_tile = []
            if not last:
                qb_tile += [0, 0]
            qb_tile += [(ntl - 4) + qb // 4 for qb in range(16)]
            emit_qodd(0)
            pendq = []
            cur_tile = -1
            nblk_in_tile = 0
            for si, (n, g, kb, first, lastkb, qfirst) in enumerate(steps):
                if qfirst:
                    if qb_tile[n] != cur_tile:
                        cur_tile = qb_tile[n]
                        nblk_in_tile = 0
                        if cur_tile == 0:
                            for lo_ in range(0, 62, 8):
                                work.append((0, (lambda lo_=lo_: build_dg(lo_, min(62, lo_ + 8)))))
                            enq_conv(0)
                    else:
                        nblk_in_tile += 1
                        if nblk_in_tile == 1 and cur_tile >= 1:
                            enq_conv(cur_tile, with_wout=cur_tile - 1)
                    if n + 1 < len(qblocks):
                        emit_qodd(n + 1)
                pti = do_s(n, g, kb)
                if len(pendq) >= 2:
                    pend = pendq.pop(0)
                    do_pv(*pend)
                    if pend[5]:
                        deferred.append((si + 3, pend[0], pend[1]))
                pendq.append((n, g, kb, pti, first, lastkb))
                while deferred and deferred[0][0] <= si:
                    _, fn_, fg_ = deferred.pop(0)
                    finalize(fn_, fg_)
                pump(2)
            for pend in pendq:
                do_pv(*pend)
                if pend[5]:
                    deferred.append((0, pend[0], pend[1]))
            for (_, fn_, fg_) in deferred:
                finalize(fn_, fg_)
            enq_wout(ntl - 1)
            while work:
                assert pump(0)
            tap('mixc1', mixc[1][:].rearrange('p a b -> p (a b)'), [128, 2048], BF16, mixc_r)
            tap('x_c1', xT[:].rearrange('p a b -> p (a b)'), [128, 8 * S], F32, x_res)
            tap('cx_c1', cxT[:].rearrange('p a b -> p (a b)'), [128, 8 * LC], F32, [cx_res])
            PB.close()
            SQ.close()

            PC = Scope(P, nc)
            xm2s = [PC.sb("xm2_%d" % i, [128, 8, 768], BF16) for i in range(2)]
            xm2_rs = [PC.res("xm2_%d" % i) for i in range(2)]
            actb = PC.sb("actb", [128, KF, 768], BF16)
            act_r = PC.res("actb")
            if b == 0 and l == 0:
                mod_layer(1, PC, 2)
            if not last:
                groups = [[("c", 0, 256), ("l", 0, 512)], [("l", 512, 512), ("l", 1024, 256)], [("l", 1280, 512), ("l", 1792, 256)]]
            else:
                groups = [[("l", 0, 512), ("l", 512, 256)], [("l", 768, 512), ("l", 1280, 256)], [("l", 1536, 512)]]
            gsegs = []
            for grp in groups:
                segs = []
                go = 0
                for (sn, s0, ln) in grp:
                    if sn == "c":
                        segs.append((lambda c, a=s0, ln=ln: cxT[:, c, a:a + ln], [cx_res], 4, go, ln))
                    else:
                        rl = [x_res[i] for i in range(s0 // 512, (s0 + ln - 1) // 512 + 1)]
                        segs.append((lambda c, a=s0, ln=ln: xT[:, c, a:a + ln], rl, b, go, ln))
                    go += ln
                gsegs.append(segs)

            def do_norm2(gi):
                xm2_, xm2_r_ = xm2s[gi % 2], xm2_rs[gi % 2]
                for (xf, xr, j, go, ln) in gsegs[gi]:
                    norm(xf, ln, (lambda c, j=j: A12[:, l, 1, j, c:c + 1]), modv(24, j), (lambda c, go=go, ln=ln, xm2_=xm2_: xm2_[:, c, go:go + ln]), xr, [xm2_r_])

            do_norm2(0)
            for gi, segs in enumerate(gsegs):
                xm2, xm2_r = xm2s[gi % 2], xm2_rs[gi % 2]
                pre_sq, pre_mm, pre_tail = [], [], []
                if gi + 1 < len(gsegs):
                    nx2, nx2_r = xm2s[(gi + 1) % 2], xm2_rs[(gi + 1) % 2]
                    for (xf_, xr_, j_, go_, ln_) in gsegs[gi + 1]:
                        a_, b_, c_ = norm_split(xf_, ln_, (lambda c, j_=j_: A12[:, l, 1, j_, c:c + 1]), modv(24, j_),
                                                (lambda c, go_=go_, ln_=ln_, nx2=nx2: nx2[:, c, go_:go_ + ln_]), xr_, [nx2_r])
                        pre_sq += a_
                        pre_mm += b_
                        pre_tail.append(c_)
                pend_mm = []
                for jj in range(11):
                    slot, slot_r = w_next(l, 7 + jj)
                    sv = slot[:, :].rearrange("p (k n) -> p k n", k=8)
                    for (xf, xr, j, go, ln) in segs:
                        for i in range(2):
                            bg = balloc()
                            mm_acc(bg, ln, (lambda k, i=i, sv=sv: sv[:, k, i * 128:(i + 1) * 128]), (lambda k, go=go, ln=ln, xm2=xm2: xm2[:, k, go:go + ln]), 8, [slot_r, xm2_r])
                            bu = balloc()
                            mm_acc(bu, ln, (lambda k, i=i, sv=sv: sv[:, k, 256 + i * 128:256 + (i + 1) * 128]), (lambda k, go=go, ln=ln, xm2=xm2: xm2[:, k, go:go + ln]), 8, [slot_r, xm2_r])
                            sg, sgr = scr()
                            P.op("act", (lambda e, bg=bg, sg=sg, ln=ln: e.activation(out=sg[:, 0:ln], in_=banks[bg][:, 0:ln], func=AF.Silu)),
                                 reads=[bank_r[bg]], writes=[sgr])
                            bfree(bg)
                            P.op("dve", (lambda e, bu=bu, sg=sg, ln=ln, go=go, jj=jj, i=i: e.tensor_tensor(
                                out=actb[:, 2 * jj + i, go:go + ln], in0=banks[bu][:, 0:ln], in1=sg[:, 0:ln], op=ALU.mult)),
                                reads=[bank_r[bu], sgr], writes=[act_r])
                            bfree(bu)
                            if jj >= 5:
                                while pend_mm:
                                    pend_mm.pop(0)()
                                if pre_sq:
                                    pre_sq.pop(0)()
                                    pend_mm.append(pre_mm.pop(0))
                while pre_sq or pend_mm:
                    while pend_mm:
                        pend_mm.pop(0)()
                    if pre_sq:
                        pre_sq.pop(0)()
                        pend_mm.append(pre_mm.pop(0))
                for t_ in pre_tail:
                    t_()
                for o in range(8):
                    slot, slot_r = w_next(l, 18 + o)
                    for (xf, xr, j, go, ln) in segs:
                        bk = balloc()
                        mm_acc(bk, ln, (lambda k, slot=slot: slot[:, k * 128:(k + 1) * 128]), (lambda k, go=go, ln=ln: actb[:, k, go:go + ln]), KF, [slot_r, act_r])
                        P.op("dve", (lambda e, o=o, bk=bk, xf=xf, j=j, ln=ln: e.scalar_tensor_tensor(
                            out=xf(o), in0=banks[bk][:, 0:ln], scalar=modT[:, l, 40 + o, j:j + 1], in1=xf(o), op0=ALU.mult, op1=ALU.add)),
                            reads=[bank_r[bk], mod_res], writes=xr)
                        bfree(bk)
            tap('x_l0', xT[:].rearrange('p a b -> p (a b)'), [128, 8 * S], F32, x_res)
            tap('cx_l0', cxT[:].rearrange('p a b -> p (a b)'), [128, 8 * LC], F32, [cx_res])
            PC.close()

        for l in range(DEPTH):
            run_layer(l)


        PF = Scope(P, nc)
        ost = [PF.sb("ost%d" % i, [128, 8, 512], F32) for i in range(2)]
        ost_r = [PF.res("ost%d" % i) for i in range(2)]
        for t in range(4):
            i = t % 2
            norm((lambda c, t=t: xT[:, c, t * 512:(t + 1) * 512]), 512, lambda c: fg32[:, c:c + 1], None,
                 (lambda c, i=i: ost[i][:, c, :]), [x_res[t]], [ost_r[i]])
            tok = P.dma("pool", "st%d" % i, (lambda e, b=b, t=t, i=i: e.dma_start(
                out=y_d[b].rearrange("p (c s) -> p c s", c=8)[:, :, t * 512:(t + 1) * 512], in_=ost[i][:, :, :])), reads=[ost_r[i]])
            store_tokens.append(tok)
        PF.close()

    for b in range(NB):
        run_sample(b)

    P.wait_tokens("pool", store_tokens)
    P.emit()
    G.close()
    return nc


_CACHE = {}


def kernel(x, c, ctx, c_ctx, w_mod, b_mod, norm1_g, norm2_g, w_in, conv_dw, conv_dw_b, conv_ln_g, conv_ln_b,
           attn_sink, pool_w, pool_scale, w_out, w_ffn_in, w_ffn_out, final_g):
    f = lambda a: np.asarray(a, dtype=np.float32)
    x, c, ctx, c_ctx = f(x), f(c), f(ctx), f(c_ctx)
    wbig = _host_weights(f(w_in), f(w_out), f(w_ffn_in), f(w_ffn_out))
    sp = _host_small(f(b_mod), f(norm1_g), f(norm2_g), f(conv_dw), f(conv_dw_b), f(conv_ln_g), f(conv_ln_b),
                     f(attn_sink), f(pool_w), f(pool_scale), f(final_g))
    ct = _host_consts()
    wm = f(w_mod)
    wmod = np.ascontiguousarray(wm.reshape(DEPTH, 8, 128, 8, 768).transpose(0, 3, 2, 1, 4)).reshape(DEPTH, 8, 128, 8 * 768)
    xT = np.ascontiguousarray(x.reshape(32, S, 8, 128).transpose(0, 3, 2, 1)).reshape(32, 128, 8 * S)
    cxT = np.ascontiguousarray(ctx.reshape(32, LC, 8, 128).transpose(0, 3, 2, 1)).reshape(32, 128, 8 * LC)
    in_maps = []
    for core in range(8):
        bs = slice(core * NB, (core + 1) * NB)
        cc = np.concatenate([c[bs], c_ctx[None, :]], 0)
        cT = np.ascontiguousarray(cc.reshape(5, 8, 128).transpose(2, 1, 0)).reshape(128, 40)
        in_maps.append({"xT": xT[bs], "cxT": cxT[bs], "cT": cT, "wmod": wmod, "sp": sp, "ct": ct, "wbig": wbig})
    if "nc" not in _CACHE:
        _CACHE["nc"] = build_program()
    nc = _CACHE["nc"]
    res = run_bass_kernel_spmd(nc, in_maps, core_ids=list(range(8)))
    yT = np.concatenate([r["yT"] for r in res.results], 0)
    out = np.ascontiguousarray(yT.reshape(32, 128, 8, S).transpose(0, 3, 2, 1)).reshape(32, S, D)
    return out.astype(np.float32)
```

```python
import contextlib
import numpy as np
import concourse.bass as bass
import concourse.mybir as mybir
from concourse.bass_utils import run_bass_kernel_spmd

F32 = mybir.dt.float32
BF16 = mybir.dt.bfloat16
ALU = mybir.AluOpType
AF = mybir.ActivationFunctionType

D = 1024
S = 2048
LC = 256
DEPTH = 2
NB = 4
DFF = 2816
KF = DFF // 128
EPS = 1e-6
NPIECE = 26
PCOLS = 4096
PIECE_N = [4096, 4096, 3072, 4096, 2048, 4096, 4096] + [4096] * 11 + [2816] * 8
LSP = 142
NSP = 2 * LSP + 8 + 512
NCT = 4096 + 256 + 2 + 32 + 2 + 128
ENGS = ("pe", "act", "dve", "pool", "sp")


class Res:
    __slots__ = ("name", "w", "r")

    def __init__(self, name, legacy):
        self.name = name
        self.w = None
        self.r = dict(legacy)


class Prog:
    def __init__(self, nc):
        self.nc = nc
        self.ins = {e: [] for e in ENGS}
        self.nchan = {}
        self.legacy = {}

    def res(self, name="r"):
        return Res(name, self.legacy)

    def free(self, resources):
        for r in resources:
            if r.w is not None:
                s, i = r.w
                self.legacy[s] = max(self.legacy.get(s, 0), i)
            for s, i in r.r.items():
                self.legacy[s] = max(self.legacy.get(s, 0), i)

    def _deps(self, reads, writes):
        deps = {}
        for r in reads:
            if r.w is not None:
                s, i = r.w
                if deps.get(s, 0) < i:
                    deps[s] = i
        for w in writes:
            if w.w is not None:
                s, i = w.w
                if deps.get(s, 0) < i:
                    deps[s] = i
            for s, i in w.r.items():
                if deps.get(s, 0) < i:
                    deps[s] = i
        return deps

    def op(self, eng, fn, reads=(), writes=()):
        deps = self._deps(reads, writes)
        idx = len(self.ins[eng]) + 1
        self.ins[eng].append((fn, deps, None))
        for r in reads:
            if r.r.get(eng, 0) < idx:
                r.r[eng] = idx
        for w in writes:
            w.w = (eng, idx)
            w.r = {}
        return (eng, idx)

    def dma(self, queue, chan, fn, reads=(), writes=()):
        deps = self._deps(reads, writes)
        k = self.nchan.get(chan, 0) + 1
        self.nchan[chan] = k
        src = "ch:" + chan
        self.ins[queue].append((fn, deps, chan))
        for r in reads:
            if r.r.get(src, 0) < k:
                r.r[src] = k
        for w in writes:
            w.w = (src, k)
            w.r = {}
        return (src, k)

    def wait_tokens(self, eng, tokens):
        deps = {}
        for s, i in tokens:
            deps[s] = max(deps.get(s, 0), i)
        self.ins[eng].append((None, deps, None))

    def emit(self):
        nc = self.nc
        needed = {e: set() for e in ENGS}
        plans = {}
        for e in ENGS:
            last = {}
            plan = []
            for n, (fn, deps, chan) in enumerate(self.ins[e]):
                waits = []
                for s, i in deps.items():
                    if s == e and e == "pe":
                        continue
                    if last.get(s, 0) >= i:
                        continue
                    last[s] = i
                    waits.append((s, i))
                    if not s.startswith("ch:"):
                        needed[s].add(i)
                plan.append(waits)
            plans[e] = plan
        rank = {}
        for e in ENGS:
            srt = sorted(needed[e])
            rank[e] = {i: k + 1 for k, i in enumerate(srt)}
        with contextlib.ExitStack() as st:
            sems = {}
            for e in ENGS:
                sems[e] = st.enter_context(nc.semaphore("s_" + e))
            for c in self.nchan:
                sems["ch:" + c] = st.enter_context(nc.semaphore("c_" + c))
            block = st.enter_context(nc.Block())

            def make(e):
                def body(eng):
                    for n, (fn, deps, chan) in enumerate(self.ins[e]):
                        for s, i in plans[e][n]:
                            if s.startswith("ch:"):
                                eng.wait_ge(sems[s], 16 * i)
                            else:
                                eng.wait_ge(sems[s], rank[s][i])
                        if fn is None:
                            continue
                        bi = fn(eng)
                        if chan is not None:
                            bi.then_inc(sems["ch:" + chan], 16)
                        elif (n + 1) in needed[e]:
                            bi.then_inc(sems[e], 1)
                return body

            block.tensor(make("pe"))
            block.scalar(make("act"))
            block.vector(make("dve"))
            block.gpsimd(make("pool"))
            block.sync(make("sp"))


SB_BASE = 16512
SB_LIMIT = 229344
_ALLOC = {"off": SB_BASE, "n": 0, "peak": 0}
_DTSZ = {}


class Scope:
    def __init__(self, P, nc):
        self.P, self.nc = P, nc
        self.start = _ALLOC["off"]
        self.rs = []

    def sb(self, name, shape, dt):
        sz = 2 if dt == BF16 else 4
        nbytes = int(np.prod(shape[1:])) * sz
        off = (_ALLOC["off"] + 63) // 64 * 64
        assert off + nbytes <= SB_LIMIT, ("SBUF overflow", name, off, nbytes)
        _ALLOC["off"] = off + nbytes
        _ALLOC["peak"] = max(_ALLOC["peak"], off + nbytes)
        _ALLOC["n"] += 1
        return self.nc.alloc_sbuf_tensor_at("%s_%d" % (name, _ALLOC["n"]), list(shape), dt, offset=off)

    def res(self, name="r"):
        r = self.P.res(name)
        self.rs.append(r)
        return r

    def close(self):
        self.P.free(self.rs)
        _ALLOC["off"] = self.start


def _rope_perm():
    perm = np.zeros(64, np.int64)
    sign = np.zeros(64, np.float32)
    for a in range(2):
        for b in range(2):
            for c in range(16):
                d = a * 32 + b * 16 + c
                perm[d] = a * 32 + (1 - b) * 16 + c
                sign[d] = -1.0 if b == 0 else 1.0
    return perm, sign


def _kmajor(w, kc):
    n = w.shape[1]
    return np.ascontiguousarray(w.reshape(kc, 128, n).transpose(1, 0, 2)).reshape(128, kc * n)


def _host_weights(w_in, w_out, w_ffn_in, w_ffn_out):
    perm, _ = _rope_perm()
    out = np.zeros((DEPTH, NPIECE, 128, PCOLS), np.float32)
    ar = np.arange(128)
    hd = ar // 64
    dd = ar % 64
    for l in range(DEPTH):
        w = w_in[l]

        def qc(c):
            return c * 128 + ar

        def qr(c):
            return c * 128 + hd * 64 + perm[dd]
        kcol = 512 + ar
        krcol = 512 + hd * 64 + perm[dd]
        vcol = 640 + ar

        def a_(c):
            return 768 + c * 128 + ar

        def g_(c):
            return 1024 + c * 128 + ar

        def pu(c):
            return 1280 + c * 128 + ar
        plist = [
            np.concatenate([qc(0), qr(0), qc(1), qr(1)]),
            np.concatenate([qc(2), qr(2), qc(3), qr(3)]),
            np.concatenate([kcol, krcol, vcol]),
            np.concatenate([a_(0), g_(0), a_(1), g_(1)]),
            np.concatenate([pu(0), pu(1)]),
        ]
        for pi, cols in enumerate(plist):
            m = _kmajor(w[:, cols], 8)
            out[l, pi, :, :m.shape[1]] = m
        for o in range(2):
            m = _kmajor(w_out[l][:, o * 512:(o + 1) * 512], 8)
            out[l, 5 + o, :, :m.shape[1]] = m
        for j in range(11):
            cols = np.concatenate([2 * j * 128 + np.arange(256), DFF + 2 * j * 128 + np.arange(256)])
            m = _kmajor(w_ffn_in[l][:, cols], 8)
            out[l, 7 + j, :, :m.shape[1]] = m
        for o in range(8):
            m = _kmajor(w_ffn_out[l][:, o * 128:(o + 1) * 128], KF)
            out[l, 18 + o, :, :m.shape[1]] = m
    return out


def _fm(v, nchunk):
    return np.ascontiguousarray(v.reshape(nchunk, 128).T)


def _host_small(b_mod, norm1_g, norm2_g, conv_dw, conv_dw_b, conv_ln_g, conv_ln_b, attn_sink,
                pool_w, pool_scale, final_g):
    sp = np.zeros((128, NSP), np.float32)
    for l in range(DEPTH):
        o = l * LSP
        sp[:, o:o + 8] = _fm(norm1_g[l], 8)
        sp[:, o + 8:o + 16] = _fm(norm2_g[l], 8)
        sp[:, o + 16:o + 64] = _fm(b_mod[l], 48)
        dw = conv_dw[l]
        for c in range(2):
            sp[:, o + 64 + c * 31:o + 64 + (c + 1) * 31] = dw[:, c * 128:(c + 1) * 128].T
        sp[:, o + 126:o + 128] = _fm(conv_dw_b[l], 2)
        sp[:, o + 128:o + 130] = _fm(conv_ln_g[l], 2)
        sp[:, o + 130:o + 132] = _fm(conv_ln_b[l], 2)
        sp[:, o + 132:o + 134] = _fm(pool_scale[l], 2)
        sp[:, o + 134:o + 142] = attn_sink[l][None, :]
        pw = pool_w[l]
        po = 2 * LSP + 8 + l * 256
        for c in range(2):
            blk = np.zeros((128, 128), np.float32)
            blk[0:64, 0:64] = pw[2 * c]
            blk[64:128, 64:128] = pw[2 * c + 1]
            sp[:, po + c * 128:po + (c + 1) * 128] = blk
    sp[:, 2 * LSP:2 * LSP + 8] = _fm(final_g, 8)
    return sp


def _host_consts():
    ct = np.zeros((128, NCT), np.float32)
    _, sign = _rope_perm()
    t = np.arange(S)
    row = (t // 64).astype(np.float32)
    col = (t % 64).astype(np.float32)
    inv = (np.float32(10000.0) ** (-np.arange(0, 32, 2, dtype=np.float32) / np.float32(32))).astype(np.float32)
    ar_ = row[:, None] * inv[None, :]
    ac_ = col[:, None] * inv[None, :]
    ang = np.concatenate([ar_, ar_, ac_, ac_], axis=-1).astype(np.float32)
    cos = np.cos(ang).astype(np.float32).T
    sin = (np.sin(ang).astype(np.float32) * sign[None, :]).T
    ct[:, 0:S] = np.concatenate([cos, cos], 0)
    ct[:, S:2 * S] = np.concatenate([sin, sin], 0)
    p = np.arange(128)[:, None]
    q = np.arange(128)[None, :]
    ct[:, 4096:4224] = (p >= q).astype(np.float32)
    ct[:, 4224:4352] = (p <= q).astype(np.float32)
    wins = np.array([[2, 8], [4, 16]])
    for half in range(2):
        for c in range(2):
            win = wins[half, c]
            ps = slice(half * 64, (half + 1) * 64)
            ct[ps, 4352 + c] = 1.0 / win
            for e in range(2):
                for i in range(8):
                    if e == 0:
                        tt = i
                        cnt = (tt + win - 1 - win // 2) - max(tt - win // 2, 0) + 1
                    else:
                        tt = -8 + i
                        cnt = min(tt + win - 1 - win // 2, -1) - (tt - win // 2) + 1
                    ct[ps, 4354 + c * 16 + e * 8 + i] = 1.0 / cnt
    ct[:, 4386] = D * EPS
    ct[:, 4387] = EPS
    ct[:, 4388:4516] = np.eye(128, dtype=np.float32)
    return ct


DEBUG = {"on": False, "taps": []}


def build_program():
    nc = bass.Bass("TRN2", target_bir_lowering=False)

    def tap(name, src, shape, dt, reads):
        if not DEBUG["on"] or any(t[0] == name for t in DEBUG["taps"]):
            return
        d = nc.dram_tensor("dbg_" + name, list(shape), dt, kind="ExternalOutput").ap()
        DEBUG["taps"].append((name, shape, dt))
        P.dma("pool", "dbg_" + name, (lambda e: e.dma_start(out=d, in_=src)), reads=reads)
        store_tokens.append(("ch:dbg_" + name, 1))
    _ALLOC["off"] = SB_BASE
    xT_d = nc.dram_tensor("xT", [NB, 128, 8 * S], F32, kind="ExternalInput").ap()
    cxT_d = nc.dram_tensor("cxT", [NB, 128, 8 * LC], F32, kind="ExternalInput").ap()
    cT_d = nc.dram_tensor("cT", [128, 40], F32, kind="ExternalInput").ap()
    wmod_d = nc.dram_tensor("wmod", [DEPTH, 8, 128, 8 * 768], F32, kind="ExternalInput").ap()
    sp_d = nc.dram_tensor("sp", [128, NSP], F32, kind="ExternalInput").ap()
    ct_d = nc.dram_tensor("ct", [128, NCT], F32, kind="ExternalInput").ap()
    wbig_d = nc.dram_tensor("wbig", [DEPTH, NPIECE, 128, PCOLS], F32, kind="ExternalInput").ap()
    y_d = nc.dram_tensor("yT", [NB, 128, 8 * S], F32, kind="ExternalOutput").ap()
    wb16_d = nc.dram_tensor("wb16", [DEPTH, NPIECE, 128, PCOLS], BF16, kind="Internal").ap()
    cs16_d = nc.dram_tensor("cs16", [128, 2 * S], BF16, kind="Internal").ap()

    P = Prog(nc)
    store_tokens = []
    G = Scope(P, nc)

    xT = G.sb("xT_s", [128, 8, S], F32)
    cxT = G.sb("cxT_s", [128, 8, LC], F32)
    x_res = [G.res("x%d" % i) for i in range(4)]
    cx_res = G.res("cx")
    csd_res = P.res("csd")
    masks = G.sb("masks", [128, 2, 512], BF16)
    ones = G.sb("ones", [128, 128], BF16)
    ident = G.sb("ident", [128, 128], BF16)
    spt = G.sb("spt", [128, 2 * LSP + 8], F32)
    ctsm = G.sb("ctsm", [128, 36], F32)
    pwb = G.sb("pwb", [128, 2, 256], BF16)
    const_res = G.res("const")
    modT = G.sb("modT", [128, 2, 48, 5], F32)
    A12 = G.sb("A12", [128, 2, 2, 5, 8], F32)
    fg32 = G.sb("fg32", [128, 8], F32)
    sexp = G.sb("sexp", [128, 2, 8], F32)
    cTb = G.sb("cTb", [128, 8, 5], BF16)
    cTb_res = G.res("cTb")
    mod_res = G.res("mod")
    ring = [G.sb("ring%d" % i, [128, PCOLS], BF16) for i in range(3)]
    ring_res = [G.res("ring%d" % i) for i in range(3)]
    NSCR = 6
    scr_t = [G.sb("scr%d" % i, [128, 544], F32) for i in range(NSCR)]
    scr_r = [G.res("scr%d" % i) for i in range(NSCR)]
    rsn_t = [G.sb("rsn%d" % i, [128, 512], F32) for i in range(1)]
    rsn_r = [G.res("rsn%d" % i) for i in range(1)]
    sq_t = [G.sb("sq%d" % i, [128, 512], BF16) for i in range(3)]
    sq_r = [G.res("sq%d" % i) for i in range(3)]
    banks = [nc.alloc_psum_tensor("bank%d" % i, [128, 512], F32) for i in range(8)]
    bank_r = [G.res("bank%d" % i) for i in range(8)]
    wscr_res = [[P.res("wscr") for _ in range(NPIECE)] for _ in range(DEPTH)]

    state = {"scr": 0, "sq": 0, "rsn": 0}
    free_banks = list(range(8))

    def scr():
        i = state["scr"]
        state["scr"] = (i + 1) % NSCR
        return scr_t[i], scr_r[i]

    def sqb():
        i = state["sq"]
        state["sq"] = (i + 1) % 3
        return sq_t[i], sq_r[i]

    def balloc():
        i = free_banks.pop(0)
        return i

    def bfree(i):
        free_banks.append(i)

    def sps(l, off, n=1):
        o = l * LSP + off
        return spt[:, o:o + n]

    P.dma("sp", "const1", lambda e: e.dma_start(out=spt[:], in_=sp_d[:, 0:2 * LSP + 8]), writes=[const_res])
    P.dma("sp", "const2", lambda e: e.dma_start(out=ctsm[:], in_=ct_d[:, 4352:4388]), writes=[const_res])

    def convert_weights(l, groups=("A", "B")):
        for grp, plist in (("A", range(0, 5)), ("B", range(5, NPIECE))):
            if grp not in groups:
                continue
            ch = "wcv%s%d" % (grp, l)
            for pi in plist:
                n = PIECE_N[pi]
                P.dma("pool", ch, (lambda e, l=l, pi=pi, n=n: e.dma_start(out=wb16_d[l, pi, :, 0:n], in_=wbig_d[l, pi, :, 0:n])),
                      writes=[wscr_res[l][pi]])
            for pi in plist:
                wscr_res[l][pi].w = ("ch:" + ch, len(plist))

    sched = []
    for b in range(NB):
        for l in range(DEPTH):
            for half in range(2):
                sched += [(l, pi) for pi in range(5)]
            sched += [(l, 5), (l, 6)]
            for grp in range(3):
                sched += [(l, 7 + j) for j in range(11)]
                sched += [(l, 18 + o) for o in range(8)]
    wstate = {"issued": 0, "used": 0}

    def w_issue_upto(k):
        while wstate["issued"] < min(k, len(sched)):
            i = wstate["issued"]
            l, pi = sched[i]
            n = PIECE_N[pi]
            s = i % 3
            P.dma("sp", "w%d" % s, (lambda e, l=l, pi=pi, n=n, s=s: e.dma_start(out=ring[s][:, 0:n], in_=wb16_d[l, pi, :, 0:n])),
                  reads=[wscr_res[l][pi]], writes=[ring_res[s]])
            wstate["issued"] += 1

    def w_next(l, pi, hold=0):
        i = wstate["used"]
        assert sched[i] == (l, pi), (i, sched[i], l, pi)
        w_issue_upto(i + 3 - hold)
        wstate["used"] += 1
        s = i % 3
        return ring[s], ring_res[s]

    def load_sample(b):
        P.dma("pool", "cxl", (lambda e, b=b: e.dma_start(out=cxT[:].rearrange("p a b -> p (a b)"), in_=cxT_d[b])), writes=[cx_res])
        for t in range(4):
            P.dma("pool", "xl%d" % t, (lambda e, b=b, t=t: e.dma_start(
                out=xT[:, :, t * 512:(t + 1) * 512],
                in_=xT_d[b].rearrange("p (c s) -> p c s", c=8)[:, :, t * 512:(t + 1) * 512])), writes=[x_res[t]])

    load_sample(0)

    PR = Scope(P, nc)
    ctb = PR.sb("ctb", [128, 4352], F32)
    ctb_res = PR.res("ctb")
    P.dma("sp", "const3", lambda e: e.dma_start(out=ctb[:], in_=ct_d[:, 0:4352]), writes=[ctb_res])
    cs0 = PR.sb("cs0", [128, 2, S], BF16)
    cs0_res = PR.res("cs0")
    P.op("act", lambda e: e.copy(out=cs0[:, 0, :], in_=ctb[:, 0:S]), reads=[ctb_res], writes=[cs0_res])
    P.op("dve", lambda e: e.tensor_copy(out=cs0[:, 1, :], in_=ctb[:, S:2 * S]), reads=[ctb_res], writes=[cs0_res])
    P.dma("sp", "const4", lambda e: e.dma_start(out=cs16_d[:, :], in_=cs0[:].rearrange("p a b -> p (a b)")), reads=[cs0_res], writes=[csd_res])
    for m in range(2):
        for r in range(4):
            P.op("dve", (lambda e, m=m, r=r: e.tensor_scalar(out=masks[:, m, r * 128:(r + 1) * 128], in0=ctb[:, 4096 + m * 128:4096 + (m + 1) * 128],
                                                              scalar1=30000.0, scalar2=-30000.0, op0=ALU.mult, op1=ALU.add)),
                 reads=[ctb_res], writes=[const_res])
    idf = PR.sb("idf", [128, 128], F32)
    idf_res = PR.res("idf")
    P.dma("sp", "const5", lambda e: e.dma_start(out=idf[:], in_=ct_d[:, 4388:4516]), writes=[idf_res])
    P.op("dve", lambda e: e.tensor_copy(out=ident[:], in_=idf[:]), reads=[idf_res], writes=[const_res])
    P.op("dve", lambda e: e.memset(ones[:], 1.0), writes=[const_res])
    pwf = PR.sb("pwf", [128, 512], F32)
    pwf_res = PR.res("pwf")
    P.dma("sp", "const6", lambda e: e.dma_start(out=pwf[:], in_=sp_d[:, 2 * LSP + 8:2 * LSP + 8 + 512]), writes=[pwf_res])
    for l in range(DEPTH):
        P.op("dve", (lambda e, l=l: e.tensor_copy(out=pwb[:, l, :], in_=pwf[:, l * 256:(l + 1) * 256])), reads=[pwf_res], writes=[const_res])
        P.op("act", (lambda e, l=l: e.activation(out=sexp[:, l, :], in_=sps(l, 134, 8), func=AF.Exp)), reads=[const_res], writes=[mod_res])
    P.op("dve", lambda e: e.tensor_scalar(out=fg32[:], in0=spt[:, 2 * LSP:2 * LSP + 8], scalar1=32.0, scalar2=None, op0=ALU.mult),
         reads=[const_res], writes=[mod_res])
    cT = PR.sb("cT_s", [128, 8, 5], F32)
    cT_res = PR.res("cT")
    P.dma("sp", "const7", lambda e: e.dma_start(out=cT[:].rearrange("p a b -> p (a b)"), in_=cT_d[:, :]), writes=[cT_res])
    P.op("act", lambda e: e.activation(out=cTb[:], in_=cT[:], func=AF.Silu), reads=[cT_res], writes=[cTb_res])

    def mod_layer(l, scope, nbuf, after_piece1=None):
        wm = [scope.sb("wm%d" % i, [128, 8, 768], BF16) for i in range(nbuf)]
        wm_res = [scope.res("wm%d" % i) for i in range(nbuf)]
        bk = balloc()
        bview = banks[bk][:, 0:240].rearrange("p (a b) -> p a b", b=5)
        for pi in range(8):
            s = pi % nbuf
            P.dma("pool", "wm%d_%d" % (l, s), (lambda e, pi=pi, s=s: e.dma_start(out=wm[s][:].rearrange("p a b -> p (a b)"), in_=wmod_d[l, pi, :, :])),
                  writes=[wm_res[s]])
            if pi == min(nbuf, 8) - 1 and after_piece1 is not None:
                after_piece1()
            for j in range(6):
                nchunk = pi * 6 + j
                for k in range(8):
                    P.op("pe", (lambda e, s=s, j=j, k=k, nchunk=nchunk: e.matmul(
                        bview[:, nchunk, :], wm[s][:, k, j * 128:(j + 1) * 128], cTb[:, k, :], start=(k == 0), stop=(k == 7))),
                        reads=[wm_res[s], cTb_res], writes=[bank_r[bk]])
        for j in range(5):
            P.op("dve", (lambda e, j=j: e.tensor_tensor(out=modT[:, l, :, j], in0=bview[:, :, j], in1=sps(l, 16, 48), op=ALU.add)),
                 reads=[bank_r[bk], const_res], writes=[mod_res])
        bfree(bk)
        for which in range(2):
            sc0 = 8 if which == 0 else 32
            for j in range(5):
                P.op("dve", (lambda e, which=which, j=j, sc0=sc0: e.tensor_scalar(
                    out=A12[:, l, which, j, :], in0=modT[:, l, sc0:sc0 + 8, j], scalar1=1.0, scalar2=32.0, op0=ALU.add, op1=ALU.mult)),
                    reads=[mod_res], writes=[mod_res])
                P.op("dve", (lambda e, which=which, j=j: e.tensor_tensor(
                    out=A12[:, l, which, j, :], in0=A12[:, l, which, j, :], in1=sps(l, 8 * which, 8), op=ALU.mult)),
                    reads=[mod_res, const_res], writes=[mod_res])

    mod_layer(0, PR, 4, after_piece1=lambda: convert_weights(0, ("A",)))
    convert_weights(0, ("B",))
    convert_weights(1)
    tap('modT', modT[:].rearrange('p a b c -> p (a b c)'), [128, 480], F32, [mod_res])
    tap('A12', A12[:].rearrange('p a b c d -> p (a b c d)'), [128, 160], F32, [mod_res])
    PR.close()

    def norm(src, TT, a_ap, sh_ap, dst, reads, writes):
        sqs, mms, tail = norm_split(src, TT, a_ap, sh_ap, dst, reads, writes)
        for c in range(8):
            sqs[c]()
            mms[c]()
        tail()

    def norm_split(src, TT, a_ap, sh_ap, dst, reads, writes, sqalloc=None):
        st = {}

        def sq_item(c):
            st[c] = (sqalloc or sqb)()
            sqt, sqr = st[c]
            P.op("act", (lambda e: e.activation(out=sqt[:, 0:TT], in_=src(c), func=AF.Square)), reads=reads, writes=[sqr])

        def mm_item(c):
            if "bk" not in st:
                st["bk"] = balloc()
            bk = st["bk"]
            sqt, sqr = st[c]
            P.op("pe", (lambda e: e.matmul(banks[bk][:, 0:TT], ones[:, :], sqt[:, 0:TT], start=(c == 0), stop=(c == 7))),
                 reads=[sqr, const_res], writes=[bank_r[bk]])

        def tail():
            norm_tail(st["bk"], src, TT, a_ap, sh_ap, dst, reads, writes)
        return [(lambda c=c: sq_item(c)) for c in range(8)], [(lambda c=c: mm_item(c)) for c in range(8)], tail

    def norm_tail(bk, src, TT, a_ap, sh_ap, dst, reads, writes):
        ri = 0
        rs, rsr = rsn_t[ri], rsn_r[ri]
        P.op("act", lambda e: e.activation(out=rs[:, 0:TT], in_=banks[bk][:, 0:TT], func=AF.Ln, bias=ctsm[:, 34:35], scale=1.0),
             reads=[bank_r[bk], const_res], writes=[rsr])
        P.op("act", lambda e: e.activation(out=rs[:, 0:TT], in_=rs[:, 0:TT], func=AF.Exp, scale=-0.5), reads=[rsr], writes=[rsr])
        bfree(bk)
        for c in range(8):
            if sh_ap is not None:
                tt, ttr = scr()
                P.op("dve", (lambda e, c=c, tt=tt: e.scalar_tensor_tensor(out=tt[:, 0:TT], in0=src(c), scalar=a_ap(c), in1=rs[:, 0:TT],
                                                                          op0=ALU.mult, op1=ALU.mult)),
                     reads=list(reads) + [rsr, mod_res], writes=[ttr])
                P.op("act", (lambda e, c=c, tt=tt: e.activation(out=dst(c), in_=tt[:, 0:TT], func=AF.Identity, bias=sh_ap(c), scale=1.0)),
                     reads=[ttr, mod_res], writes=writes)
            else:
                P.op("dve", (lambda e, c=c: e.scalar_tensor_tensor(out=dst(c), in0=src(c), scalar=a_ap(c), in1=rs[:, 0:TT],
                                                                   op0=ALU.mult, op1=ALU.mult)),
                     reads=list(reads) + [rsr, mod_res], writes=writes)

    def mm_acc(bk, ncol, lhs_fn, rhs_fn, nk, reads):
        for k in range(nk):
            P.op("pe", (lambda e, k=k: e.matmul(banks[bk][:, 0:ncol], lhs_fn(k), rhs_fn(k), start=(k == 0), stop=(k == nk - 1))),
                 reads=reads, writes=[bank_r[bk]])


    def run_sample(b):
        if b > 0:
            load_sample(b)

        def run_layer(l):
            last = (l == DEPTH - 1)
            SQ = Scope(P, nc)
            qT = SQ.sb("qT", [128, 4, S], BF16)
            q_res = [SQ.res("q%d" % i) for i in range(16)]
            kTs = SQ.sb("kTs", [128, 2, S], BF16)
            k_res = [SQ.res("k%d" % i) for i in range(4)]
            Va = SQ.sb("Va", [128, 16, 2, 128], BF16)
            v_res = [SQ.res("v%d" % i) for i in range(4)]
            hb = SQ.sb("hb", [128, 2, S + 30], BF16)
            h_res = [SQ.res("h%d" % i) for i in range(4)]
            pub = SQ.sb("pub", [128, 2, S + 16], BF16)
            pu_res = [SQ.res("pu%d" % i) for i in range(4)]
            qTc = SQ.sb("qTc", [128, 4, LC], BF16)
            qc_res = [SQ.res("qc%d" % i) for i in range(2)]
            kTc = SQ.sb("kTc", [128, 2, LC], BF16)
            kc_res = SQ.res("kc")
            Vc = SQ.sb("Vc", [128, 2, 2, 128], BF16)
            vc_res = SQ.res("vc")
            hc = SQ.sb("hc", [128, 2, LC + 30], BF16)
            hc_res = SQ.res("hc")
            puc = SQ.sb("puc", [128, 2, LC + 16], BF16)
            puc_res = SQ.res("puc")
            P.op("dve", lambda e: e.memset(hb[:, :, 0:15], 0.0), writes=[h_res[0]])
            P.op("dve", lambda e: e.memset(hb[:, :, S + 15:S + 30], 0.0), writes=[h_res[3]])
            P.op("dve", lambda e: e.memset(pub[:, :, 0:8], 0.0), writes=[pu_res[0]])
            P.op("dve", lambda e: e.memset(pub[:, :, S + 8:S + 16], 0.0), writes=[pu_res[3]])
            P.op("dve", lambda e: e.memset(Va[:, :, :, 64:128], 1.0), writes=v_res)
            P.op("dve", lambda e: e.memset(kTs[64:128, :, :], 0.0), writes=k_res)
            P.op("dve", lambda e: e.memset(kTc[64:128, :, :], 0.0), writes=[kc_res])
            P.op("dve", lambda e: e.memset(Vc[:, :, :, 64:128], 1.0), writes=[vc_res])
            if not last:
                P.op("dve", lambda e: e.memset(hc[:, :, 0:15], 0.0), writes=[hc_res])
                P.op("dve", lambda e: e.memset(hc[:, :, LC + 15:LC + 30], 0.0), writes=[hc_res])
                P.op("dve", lambda e: e.memset(puc[:, :, 0:8], 0.0), writes=[puc_res])
                P.op("dve", lambda e: e.memset(puc[:, :, LC + 8:LC + 16], 0.0), writes=[puc_res])

            def modv(off, j):
                return lambda c: modT[:, l, off + c, j:j + 1]

            PA = Scope(P, nc)
            xm = [PA.sb("xm%d" % i, [128, 8, 512], BF16) for i in range(2)]
            xm_res = [PA.res("xm%d" % i) for i in range(2)]
            xmc = PA.sb("xmc", [128, 8, LC], BF16)
            xmc_res = PA.res("xmc")
            cs = PA.sb("cs", [128, 2, S], BF16)
            cs_res = PA.res("cs")
            P.dma("sp", "csl", lambda e: e.dma_start(out=cs[:].rearrange("p a b -> p (a b)"), in_=cs16_d[:, :]), reads=[csd_res], writes=[cs_res])

            def proj_tile(pi, slot, slot_r, xmt, xmr, TT, t0, is_ctx, tix):
                ncols = PIECE_N[pi] // 8
                sv = slot[:, 0:PIECE_N[pi]].rearrange("p (k n) -> p k n", k=8)

                def acc(col0, width=128):
                    bk = balloc()
                    mm_acc(bk, TT, lambda k: sv[:, k, col0:col0 + width], lambda k: xmt[:, k, 0:TT], 8, [slot_r, xmr])
                    return bk
                if pi in (0, 1):
                    for i in range(2):
                        c = 2 * pi + i
                        bq = acc(i * 256)
                        if is_ctx:
                            P.op("act", (lambda e, c=c, bq=bq: e.copy(out=qTc[:, c, :], in_=banks[bq][:, 0:TT])),
                                 reads=[bank_r[bq]], writes=qc_res)
                            bfree(bq)
                            continue
                        br = acc(i * 256 + 128)
                        t1, t1r = scr()
                        t2, t2r = scr()
                        P.op("dve", (lambda e, bq=bq, t1=t1: e.tensor_tensor(out=t1[:, 0:TT], in0=banks[bq][:, 0:TT], in1=cs[:, 0, t0:t0 + TT], op=ALU.mult)),
                             reads=[bank_r[bq], cs_res], writes=[t1r])
                        P.op("dve", (lambda e, br=br, t2=t2: e.tensor_tensor(out=t2[:, 0:TT], in0=banks[br][:, 0:TT], in1=cs[:, 1, t0:t0 + TT], op=ALU.mult)),
                             reads=[bank_r[br], cs_res], writes=[t2r])
                        bfree(bq)
                        bfree(br)
                        P.op("dve", (lambda e, c=c, t1=t1, t2=t2: e.tensor_tensor(out=qT[:, c, t0:t0 + TT], in0=t1[:, 0:TT], in1=t2[:, 0:TT], op=ALU.add)),
                             reads=[t1r, t2r], writes=q_res[4 * tix:4 * tix + 4])
                elif pi == 2:
                    bq = acc(0)
                    kt, ktr = sqb()
                    if is_ctx:
                        P.op("act", (lambda e, bq=bq, kt=kt: e.copy(out=kt[:, 0:TT], in_=banks[bq][:, 0:TT])), reads=[bank_r[bq]], writes=[ktr])
                        bfree(bq)
                    else:
                        br = acc(128)
                        t1, t1r = scr()
                        t2, t2r = scr()
                        P.op("dve", (lambda e, bq=bq, t1=t1: e.tensor_tensor(out=t1[:, 0:TT], in0=banks[bq][:, 0:TT], in1=cs[:, 0, t0:t0 + TT], op=ALU.mult)),
                             reads=[bank_r[bq], cs_res], writes=[t1r])
                        P.op("dve", (lambda e, br=br, t2=t2: e.tensor_tensor(out=t2[:, 0:TT], in0=banks[br][:, 0:TT], in1=cs[:, 1, t0:t0 + TT], op=ALU.mult)),
                             reads=[bank_r[br], cs_res], writes=[t2r])
                        bfree(bq)
                        bfree(br)
                        P.op("dve", (lambda e, kt=kt, t1=t1, t2=t2: e.tensor_tensor(out=kt[:, 0:TT], in0=t1[:, 0:TT], in1=t2[:, 0:TT], op=ALU.add)),
                             reads=[t1r, t2r], writes=[ktr])
                    kdst = kTc if is_ctx else kTs
                    kw = [kc_res] if is_ctx else [k_res[tix]]
                    for g in range(2):
                        P.op("act", (lambda e, g=g, kt=kt, kdst=kdst: e.copy(out=kdst[0:64, g, t0:t0 + TT], in_=kt[g * 64:(g + 1) * 64, 0:TT])),
                             reads=[ktr], writes=kw)
                    nblk = TT // 128
                    bv = balloc()
                    for blk in range(nblk):
                        for k in range(8):
                            P.op("pe", (lambda e, blk=blk, k=k: e.matmul(banks[bv][:, blk * 128:(blk + 1) * 128], xmt[:, k, blk * 128:(blk + 1) * 128],
                                                                         sv[:, k, 256:384], start=(k == 0), stop=(k == 7))),
                                 reads=[slot_r, xmr], writes=[bank_r[bv]])
                    vdst = Vc if is_ctx else Va
                    kb0 = t0 // 128
                    P.op("act", (lambda e, bv=bv, vdst=vdst: e.copy(
                        out=vdst[:, kb0:kb0 + nblk, :, 0:64],
                        in_=banks[bv][:, 0:TT].rearrange("p (b g d) -> p b g d", g=2, d=64))),
                        reads=[bank_r[bv]], writes=([vc_res] if is_ctx else [v_res[tix]]))
                    bfree(bv)
                elif pi == 3:
                    hdst = hc if is_ctx else hb
                    hw = [hc_res] if is_ctx else [h_res[tix]]
                    for c in range(2):
                        ba = acc(c * 256)
                        bg = acc(c * 256 + 128)
                        sg, sgr = scr()
                        P.op("act", (lambda e, bg=bg, sg=sg: e.activation(out=sg[:, 0:TT], in_=banks[bg][:, 0:TT], func=AF.Sigmoid)),
                             reads=[bank_r[bg]], writes=[sgr])
                        bfree(bg)
                        P.op("dve", (lambda e, c=c, ba=ba, sg=sg, hdst=hdst: e.tensor_tensor(out=hdst[:, c, 15 + t0:15 + t0 + TT], in0=banks[ba][:, 0:TT],
                                                                                      in1=sg[:, 0:TT], op=ALU.mult)),
                             reads=[bank_r[ba], sgr], writes=hw)
                        bfree(ba)
                elif pi == 4:
                    pdst = puc if is_ctx else pub
                    pw_ = [puc_res] if is_ctx else [pu_res[tix]]
                    for c in range(2):
                        bp = acc(c * 128)
                        P.op("act", (lambda e, c=c, bp=bp, pdst=pdst: e.copy(out=pdst[:, c, 8 + t0:8 + t0 + TT], in_=banks[bp][:, 0:TT])),
                             reads=[bank_r[bp]], writes=pw_)
                        bfree(bp)

            norm(lambda c: cxT[:, c, :], LC, lambda c: A12[:, l, 0, 4, c:c + 1], modv(0, 4),
                 lambda c: xmc[:, c, :], [cx_res], [xmc_res])
            nsb = [(PA.sb("nsb%d" % i, [128, 512], BF16), PA.res("nsb%d" % i)) for i in range(4)]
            nsb_i = {"i": 0}

            def nsqb():
                nsb_i["i"] = (nsb_i["i"] + 1) % 4
                return nsb[nsb_i["i"]]

            for half in range(2):
                nsq, nmm, ntail, npend = [], [], [], []
                for i in range(2):
                    t = 2 * half + i
                    if half == 0:
                        norm((lambda c, t=t: xT[:, c, t * 512:(t + 1) * 512]), 512, lambda c: A12[:, l, 0, b, c:c + 1], modv(0, b),
                             (lambda c, i=i: xm[i][:, c, :]), [x_res[t]], [xm_res[i]])
                        a_, b_, c_ = norm_split((lambda c, t=t + 2: xT[:, c, t * 512:(t + 1) * 512]), 512, lambda c: A12[:, l, 0, b, c:c + 1], modv(0, b),
                                                (lambda c, i=i: xm[i][:, c, :]), [x_res[t + 2]], [xm_res[i]], sqalloc=nsqb)
                        nsq += a_
                        nmm += b_
                        ntail.append(c_)
                for pi in range(5):
                    slot, slot_r = w_next(l, pi)
                    if half == 0 and ((not last) or pi == 2):
                        proj_tile(pi, slot, slot_r, xmc, xmc_res, LC, 0, True, 0)
                    for i in range(2):
                        t = 2 * half + i
                        proj_tile(pi, slot, slot_r, xm[i], xm_res[i], 512, t * 512, False, t)
                        if half == 0:
                            while npend:
                                npend.pop(0)()
                            for _ in range(2):
                                if nsq:
                                    nsq.pop(0)()
                                    npend.append(nmm.pop(0))
                            if pi == 4:
                                assert not nsq and not npend
                                ntail[i]()
            tap('xm1', xm[1][:].rearrange('p a b -> p (a b)'), [128, 4096], BF16, xm_res)
            tap('qT', qT[:].rearrange('p a b -> p (a b)'), [128, 4 * S], BF16, q_res)
            tap('kTs', kTs[0:64].rearrange('p a b -> p (a b)'), [64, 2 * S], BF16, k_res)
            tap('Va', Va[:].rearrange('p a b c -> p (a b c)'), [128, 4096], BF16, v_res)
            tap('hb', hb[:].rearrange('p a b -> p (a b)'), [128, 2 * (S + 30)], BF16, h_res)
            tap('pub', pub[:].rearrange('p a b -> p (a b)'), [128, 2 * (S + 16)], BF16, pu_res)
            tap('kTc', kTc[0:64].rearrange('p a b -> p (a b)'), [64, 2 * LC], BF16, [kc_res])
            tap('qTc', qTc[:].rearrange('p a b -> p (a b)'), [128, 4 * LC], BF16, qc_res)
            PA.close()

            PB = Scope(P, nc)
            PT = [PB.sb("PT%d" % i, [128, 512], BF16) for i in range(4)]
            PT_r = [PB.res("PT%d" % i) for i in range(4)]
            ptst = {"i": 0}
            qg = [PB.sb("qg%d" % i, [128, 8, 128], BF16) for i in range(2)]
            qodd_r = [PB.res("qg%d" % i) for i in range(2)]
            for i_ in range(2):
                P.op("dve", (lambda e, i_=i_: e.memset(qg[i_][64:128, :, :], 0.0)), writes=[qodd_r[i_]])
            otmp = PB.sb("otmp", [64, 2, 128], BF16)
            otmp_r = PB.res("otmp")
            mixc = [PB.sb("mixc%d" % i, [128, 4, 512], BF16) for i in range(2)]
            mixc_r = [PB.res("mixc%d" % i) for i in range(2)]
            dg = PB.sb("dg", [128, 62, 128], BF16)
            dg_r = PB.res("dg")
            def build_dg(lo, hi):
                for ci in range(lo, hi):
                    if True:
                        P.op("dve", (lambda e, ci=ci: e.tensor_scalar(out=dg[:, ci, :], in0=ident[:, :], scalar1=sps(l, 64 + ci), scalar2=None, op0=ALU.mult)),
                             reads=[const_res], writes=[dg_r])
                    else:
                        P.op("act", (lambda e, ci=ci: e.activation(out=dg[:, ci, :], in_=ident[:, :], func=AF.Identity, scale=sps(l, 64 + ci))),
                             reads=[const_res], writes=[dg_r])

            def ctx_kb(j):
                return (lambda g: kTc[:, g, j * 128:(j + 1) * 128], lambda g: Vc[:, j, g, :], None, [kc_res, vc_res])

            def loc_kb(j, m):
                return (lambda g: kTs[:, g, j * 128:(j + 1) * 128], lambda g: Va[:, j, g, :], m, [k_res[j // 4], v_res[j // 4]])

            qblocks = []
            if not last:
                for qb in range(2):
                    qblocks.append((qTc, qc_res[qb], qb, [ctx_kb(0), ctx_kb(1)]))
            for qb in range(16):
                kbs = [ctx_kb(0), ctx_kb(1)]
                if qb > 0:
                    kbs.append(loc_kb(qb - 1, 0))
                kbs.append(loc_kb(qb, None))
                if qb < 15:
                    kbs.append(loc_kb(qb + 1, 1))
                qblocks.append((qT, q_res[qb], qb, kbs))

            def emit_qodd(n):
                qbuf, qr, qb, kbs = qblocks[n]
                q0 = qb * 128
                qg4 = qg[n % 2][0:64, :, :].rearrange("p (c par) q -> p c par q", par=2)
                P.op("dve", lambda e: e.tensor_copy(out=qg4[:, :, 0, :], in_=qbuf[0:64, :, q0:q0 + 128]), reads=[qr], writes=[qodd_r[n % 2]])
                P.op("dve", lambda e: e.tensor_copy(out=qg4[:, :, 1, :], in_=qbuf[64:128, :, q0:q0 + 128]), reads=[qr], writes=[qodd_r[n % 2]])

            steps = []
            for n, (qbuf, qr, qb, kbs) in enumerate(qblocks):
                for g in range(2):
                    for ki, kb in enumerate(kbs):
                        steps.append((n, g, kb, ki == 0, ki == len(kbs) - 1, g == 0 and ki == 0))
            obank = {}
            deferred = []

            def do_s(n, g, kb):
                qbuf, qr, qb, kbs = qblocks[n]
                q0 = qb * 128
                bs = balloc()
                masked = kb[2] is not None
                if masked:
                    P.op("pe", (lambda e, bs=bs, m=kb[2]: e.matmul(banks[bs][:, :], ident[:, :], masks[:, m, :], start=True, stop=False, skip_group_check=True)),
                         reads=[const_res], writes=[bank_r[bs]])
                rhs = qg[n % 2][:, 4 * g:4 * g + 4, :].rearrange("p h q -> p (h q)")
                P.op("pe", (lambda e, rhs=rhs, bs=bs, kb=kb, g=g, masked=masked: e.matmul(
                    banks[bs][:, :], kb[0](g), rhs, start=(not masked), stop=True, skip_group_check=True)),
                    reads=[qodd_r[n % 2]] + kb[3], writes=[bank_r[bs]])
                pti = ptst["i"]
                ptst["i"] = (pti + 1) % 4
                P.op("act", (lambda e, bs=bs, pti=pti: e.activation(out=PT[pti][:, :], in_=banks[bs][:, :], func=AF.Exp, scale=0.125)),
                     reads=[bank_r[bs]], writes=[PT_r[pti]])
                bfree(bs)
                return pti

            def do_pv(n, g, kb, pti, first, lastkb):
                if first:
                    obank[(n, g)] = balloc()
                ob = obank[(n, g)]
                P.op("pe", (lambda e: e.matmul(banks[ob][:, :], kb[1](g), PT[pti][:, :], start=first, stop=lastkb, skip_group_check=True)),
                     reads=[PT_r[pti]] + kb[3], writes=[bank_r[ob]])

            def finalize(n, g):
                qbuf, qr, qb, kbs = qblocks[n]
                q0 = qb * 128
                ob = obank[(n, g)]
                ssb, ssr = rsn_t[0], rsn_r[0]
                for hl in range(4):
                    h = 4 * g + hl
                    P.op("dve", (lambda e, hl=hl, h=h: e.tensor_scalar(out=ssb[0:64, hl * 128:(hl + 1) * 128], in0=banks[ob][64:128, hl * 128:(hl + 1) * 128],
                                                                       scalar1=sexp[64:128, l, h:h + 1], scalar2=None, op0=ALU.add)),
                         reads=[bank_r[ob], mod_res], writes=[ssr])
                P.op("act", lambda e: e.activation(out=ssb[0:64, 0:512], in_=ssb[0:64, 0:512], func=AF.Ln), reads=[ssr], writes=[ssr])
                P.op("act", lambda e: e.activation(out=ssb[0:64, 0:512], in_=ssb[0:64, 0:512], func=AF.Exp, scale=-1.0), reads=[ssr], writes=[ssr])
                ob4 = banks[ob][0:64, :].rearrange("p (a two q) -> p a two q", a=2, two=2)
                ss4 = ssb[0:64, 0:512].rearrange("p (a two q) -> p a two q", a=2, two=2)
                P.op("dve", lambda e: e.tensor_tensor(out=qbuf[0:64, 2 * g:2 * g + 2, q0:q0 + 128], in0=ob4[:, :, 0, :], in1=ss4[:, :, 0, :], op=ALU.mult),
                     reads=[bank_r[ob], ssr], writes=[qr])
                P.op("dve", lambda e: e.tensor_tensor(out=qbuf[64:128, 2 * g:2 * g + 2, q0:q0 + 128], in0=ob4[:, :, 1, :], in1=ss4[:, :, 1, :], op=ALU.mult),
                     reads=[bank_r[ob], ssr], writes=[qr])
                bfree(ob)

            def conv_mm(hbuf, hreads, TT, t0):
                cbanks = []
                for c in range(2):
                    bc = balloc()
                    cbanks.append(bc)
                    for tap in range(31):
                        P.op("pe", (lambda e, c=c, tap=tap, bc=bc: e.matmul(banks[bc][:, 0:TT], dg[:, c * 31 + tap, :], hbuf[:, c, t0 + tap:t0 + tap + TT],
                                                                          start=(tap == 0), stop=(tap == 30))),
                             reads=hreads + [dg_r], writes=[bank_r[bc]])
                return cbanks

            def pool_stages(pbuf, preads, n, TT, t0, mix, mix_r):
                W = TT + 16
                st = {}

                def chain(c):
                    X = lambda lo, hi: pbuf[:, c, t0 + lo:t0 + hi]
                    T1, T1r = scr()
                    P.op("dve", (lambda e: e.tensor_tensor(out=T1[:, 1:W - 1], in0=X(0, W - 2), in1=X(1, W - 1), op=ALU.add)),
                         reads=preads, writes=[T1r])
                    yv, yvr = sqb()
                    st[c] = (yv, yvr)
                    if c == 0:
                        T2, T2r = scr()
                        P.op("dve", (lambda e: e.tensor_tensor(out=T2[64:128, 0:TT], in0=T1[64:128, 7:7 + TT], in1=T1[64:128, 9:9 + TT], op=ALU.add)),
                             reads=[T1r], writes=[T2r])
                        srcs = [(T1, T1r, 8, 0), (T2, T2r, 0, 64)]
                    else:
                        T2, T2r = scr()
                        T3, T3r = scr()
                        P.op("dve", (lambda e: e.tensor_tensor(out=T2[:, 3:W - 1], in0=T1[:, 1:W - 3], in1=T1[:, 3:W - 1], op=ALU.add)),
                             reads=[T1r], writes=[T2r])
                        P.op("dve", (lambda e: e.tensor_tensor(out=T3[:, 7:W - 1], in0=T2[:, 3:W - 5], in1=T2[:, 7:W - 1], op=ALU.add)),
                             reads=[T2r], writes=[T3r])
                        P.op("dve", (lambda e: e.tensor_tensor(out=T1[64:128, 0:TT], in0=T3[64:128, 7:7 + TT], in1=T3[64:128, 15:15 + TT], op=ALU.add)),
                             reads=[T3r, T1r], writes=[T1r])
                        srcs = [(T3, T3r, 11, 0), (T1, T1r, 0, 64)]
                    for (Wt, Wr, off, p0) in srcs:
                        ps_ = slice(p0, p0 + 64)
                        P.op("dve", (lambda e, Wt=Wt, off=off, ps_=ps_: e.scalar_tensor_tensor(
                            out=yv[ps_, 0:TT], in0=Wt[ps_, off:off + TT], scalar=ctsm[ps_, c:c + 1], in1=pbuf[ps_, c, t0 + 8:t0 + 8 + TT],
                            op0=ALU.mult, op1=ALU.subtract)), reads=[Wr, const_res] + preads, writes=[yvr])
                        edges = []
                        if t0 == 0:
                            edges.append((0, 0))
                        if t0 + TT == n:
                            edges.append((1, TT - 8))
                        for (ei, eo) in edges:
                            et, etr = scr()
                            P.op("dve", (lambda e, Wt=Wt, off=off, ps_=ps_, et=et, ei=ei, eo=eo: e.tensor_tensor(
                                out=et[ps_, 0:8], in0=Wt[ps_, off + eo:off + eo + 8], in1=ctsm[ps_, 2 + c * 16 + ei * 8:2 + c * 16 + ei * 8 + 8], op=ALU.mult)),
                                reads=[Wr, const_res], writes=[etr])
                            P.op("dve", (lambda e, ps_=ps_, et=et, eo=eo: e.tensor_tensor(
                                out=yv[ps_, eo:eo + 8], in0=et[ps_, 0:8], in1=pbuf[ps_, c, t0 + 8 + eo:t0 + 8 + eo + 8], op=ALU.subtract)),
                                reads=[etr, yvr] + preads, writes=[yvr])

                def mmev(c):
                    yv, yvr = st[c]
                    bp = balloc()
                    P.op("pe", (lambda e: e.matmul(banks[bp][:, 0:TT], pwb[:, l, c * 128:(c + 1) * 128], yv[:, 0:TT], start=True, stop=True)),
                         reads=[yvr, const_res], writes=[bank_r[bp]])
                    P.op("dve", (lambda e: e.tensor_scalar(out=mix[:, 2 + c, 0:TT], in0=banks[bp][:, 0:TT], scalar1=sps(l, 132 + c), scalar2=None, op0=ALU.mult)),
                         reads=[bank_r[bp], const_res], writes=[mix_r])
                    bfree(bp)
                return [(0, lambda: chain(0)), (1, lambda: mmev(0)), (0, lambda: chain(1)), (1, lambda: mmev(1))]

            def ln_stages(get_cbanks, TT, mix, mix_r):
                st = {}

                def L1():
                    cbanks = get_cbanks()
                    accs = []
                    for c in range(2):
                        bc = cbanks[c]
                        acc, accr = scr()
                        accs.append((acc, accr))
                        P.op("dve", (lambda e, c=c, acc=acc, bc=bc: e.tensor_scalar(out=acc[:, 0:TT], in0=banks[bc][:, 0:TT], scalar1=sps(l, 126 + c),
                                                                                   scalar2=None, op0=ALU.add)),
                             reads=[bank_r[bc], const_res], writes=[accr])
                        bfree(bc)
                    st["accs"] = accs

                def L2():
                    b1 = balloc()
                    b2 = balloc()
                    st["b"] = (b1, b2)
                    for c in range(2):
                        acc, accr = st["accs"][c]
                        s1, s1r = sqb()
                        s2, s2r = sqb()
                        P.op("dve", (lambda e, acc=acc, s1=s1: e.tensor_copy(out=s1[:, 0:TT], in_=acc[:, 0:TT])), reads=[accr], writes=[s1r])
                        P.op("act", (lambda e, acc=acc, s2=s2: e.activation(out=s2[:, 0:TT], in_=acc[:, 0:TT], func=AF.Square)), reads=[accr], writes=[s2r])
                        P.op("pe", (lambda e, c=c, s1=s1: e.matmul(banks[b1][:, 0:TT], ones[:, :], s1[:, 0:TT], start=(c == 0), stop=(c == 1))),
                             reads=[s1r, const_res], writes=[bank_r[b1]])
                        P.op("pe", (lambda e, c=c, s2=s2: e.matmul(banks[b2][:, 0:TT], ones[:, :], s2[:, 0:TT], start=(c == 0), stop=(c == 1))),
                             reads=[s2r, const_res], writes=[bank_r[b2]])

                def L3():
                    b1, b2 = st["b"]
                    mu, mur = scr()
                    rs, rsr = scr()
                    st["mu"] = (mu, mur)
                    st["rs"] = (rs, rsr)
                    P.op("dve", lambda e: e.tensor_scalar(out=mu[:, 0:TT], in0=banks[b1][:, 0:TT], scalar1=1.0 / 256, scalar2=None, op0=ALU.mult),
                         reads=[bank_r[b1]], writes=[mur])
                    P.op("dve", lambda e: e.tensor_tensor(out=rs[:, 0:TT], in0=mu[:, 0:TT], in1=mu[:, 0:TT], op=ALU.mult), reads=[mur], writes=[rsr])
                    P.op("dve", lambda e: e.scalar_tensor_tensor(out=rs[:, 0:TT], in0=banks[b2][:, 0:TT], scalar=1.0 / 256, in1=rs[:, 0:TT],
                                                                 op0=ALU.mult, op1=ALU.subtract), reads=[bank_r[b2], rsr], writes=[rsr])
                    bfree(b1)
                    bfree(b2)

                def L4():
                    rs, rsr = st["rs"]
                    P.op("act", lambda e: e.activation(out=rs[:, 0:TT], in_=rs[:, 0:TT], func=AF.Ln, bias=ctsm[:, 35:36], scale=1.0),
                         reads=[rsr, const_res], writes=[rsr])
                    P.op("act", lambda e: e.activation(out=rs[:, 0:TT], in_=rs[:, 0:TT], func=AF.Exp, scale=-0.5), reads=[rsr], writes=[rsr])

                def L5():
                    mu, mur = st["mu"]
                    rs, rsr = st["rs"]
                    for c in range(2):
                        acc, accr = st["accs"][c]
                        P.op("dve", (lambda e, acc=acc: e.tensor_tensor(out=acc[:, 0:TT], in0=acc[:, 0:TT], in1=mu[:, 0:TT], op=ALU.subtract)),
                             reads=[accr, mur], writes=[accr])
                        P.op("dve", (lambda e, acc=acc: e.tensor_tensor(out=acc[:, 0:TT], in0=acc[:, 0:TT], in1=rs[:, 0:TT], op=ALU.mult)),
                             reads=[accr, rsr], writes=[accr])

                def L6():
                    for c in range(2):
                        acc, accr = st["accs"][c]
                        P.op("act", (lambda e, c=c, acc=acc: e.activation(out=mix[:, c, 0:TT], in_=acc[:, 0:TT], func=AF.Silu, bias=sps(l, 130 + c),
                                                                          scale=sps(l, 128 + c))), reads=[accr, const_res], writes=[mix_r])
                return [(0, L1), (2, L2), (0, L3), (0, L4), (0, L5), (0, L6)]

            def wout_tile(slots, TT, abuf, areads, t0, mix, mix_r, xdst, xres, j, ocs=range(8)):
                for oc in ocs:
                    slot, slot_r = slots[oc // 4]
                    sv = slot[:, :].rearrange("p (k n) -> p k n", k=8)
                    col0 = (oc % 4) * 128
                    bk = balloc()
                    for k in range(8):
                        rhs = abuf[:, k, t0:t0 + TT] if k < 4 else mix[:, k - 4, 0:TT]
                        P.op("pe", (lambda e, k=k, rhs=rhs, sv=sv, col0=col0, bk=bk: e.matmul(banks[bk][:, 0:TT], sv[:, k, col0:col0 + 128], rhs,
                                                                                       start=(k == 0), stop=(k == 7))),
                             reads=[slot_r, mix_r] + areads, writes=[bank_r[bk]])
                    P.op("dve", (lambda e, oc=oc, bk=bk: e.scalar_tensor_tensor(out=xdst(oc), in0=banks[bk][:, 0:TT], scalar=modT[:, l, 16 + oc, j:j + 1],
                                                                               in1=xdst(oc), op0=ALU.mult, op1=ALU.add)),
                         reads=[bank_r[bk], mod_res], writes=xres)
                    bfree(bk)

            tap('attn', qT[:].rearrange('p a b -> p (a b)'), [128, 4 * S], BF16, q_res)
            tap('attnc', qTc[:].rearrange('p a b -> p (a b)'), [128, 4 * LC], BF16, qc_res)
            wslots = [w_next(l, 5), w_next(l, 6, hold=1)]
            tiles = []
            if not last:
                tiles.append(dict(hbuf=hc, hreads=[hc_res], pbuf=puc, preads=[puc_res], n=LC, TT=LC, t0=0, abuf=qTc, areads=qc_res,
                                  xdst=(lambda oc: cxT[:, oc, :]), xres=[cx_res], j=4))
            for t in range(4):
                nb_h = [h_res[i] for i in range(max(0, t - 1), min(4, t + 2))]
                nb_p = [pu_res[i] for i in range(max(0, t - 1), min(4, t + 2))]
                tiles.append(dict(hbuf=hb, hreads=nb_h, pbuf=pub, preads=nb_p, n=S, TT=512, t0=t * 512, abuf=qT, areads=q_res[4 * t:4 * t + 4],
                                  xdst=(lambda oc, t=t: xT[:, oc, t * 512:(t + 1) * 512]), xres=[x_res[t]], j=b))
            from collections import deque
            work = deque()
            cbs = {}

            def conv_items(i):
                T = tiles[i]
                mx, mxr = mixc[i % 2], mixc_r[i % 2]

                def f_cm():
                    cbs[i] = conv_mm(T["hbuf"], T["hreads"], T["TT"], T["t0"])
                ps = pool_stages(T["pbuf"], T["preads"], T["n"], T["TT"], T["t0"], mx, mxr)
                ls = ln_stages((lambda: cbs[i]), T["TT"], mx, mxr)
                return [(2, f_cm), ps[0], ps[2], ps[1], ps[3]] + ls

            def wout_items(i):
                T = tiles[i]
                return [(1, (lambda oc=oc: wout_tile(wslots, T["TT"], T["abuf"], T["areads"], T["t0"], mixc[i % 2], mixc_r[i % 2],
                                                     T["xdst"], T["xres"], T["j"], ocs=[oc]))) for oc in range(8)]

            def enq_conv(i, with_wout=None):
                ci = conv_items(i)
                wi = wout_items(with_wout) if with_wout is not None else []
                order = [0, 'w', 1, 'w', 2, 'w', 3, 'w', 4, 5, 'w', 6, 'w', 7, 8, 'w', 9, 'w', 10]
                if with_wout is None:
                    order = [1, 2, 0, 3, 4, 5, 6, 7, 8, 9, 10]
                for o in order:
                    if o == 'w':
                        if wi:
                            work.append(wi.pop(0))
                    else:
                        work.append(ci[o])
                for it in wi:
                    work.append(it)

            def enq_wout(i):
                for it in wout_items(i):
                    work.append(it)

            def pump(reserve):
                if work and len(free_banks) >= work[0][0] + reserve:
                    work.popleft()[1]()
                    return True
                return False

            ntl = len(tiles)
            qb_tile = []
            if not last:
                qb_tile += [0, 0]
            qb_tile += [(ntl - 4) + qb // 4 for qb in range(16)]
            emit_qodd(0)
            pendq = []
            cur_tile = -1
            nblk_in_tile = 0
            for si, (n, g, kb, first, lastkb, qfirst) in enumerate(steps):
                if qfirst:
                    if qb_tile[n] != cur_tile:
                        cur_tile = qb_tile[n]
                        nblk_in_tile = 0
                        if cur_tile == 0:
                            for lo_ in range(0, 62, 8):
                                work.append((0, (lambda lo_=lo_: build_dg(lo_, min(62, lo_ + 8)))))
                            enq_conv(0)
                    else:
                        nblk_in_tile += 1
                        if nblk_in_tile == 1 and cur_tile >= 1:
                            enq_conv(cur_tile, with_wout=cur_tile - 1)
                    if n + 1 < len(qblocks):
                        emit_qodd(n + 1)
                pti = do_s(n, g, kb)
                if len(pendq) >= 2:
                    pend = pendq.pop(0)
                    do_pv(*pend)
                    if pend[5]:
                        deferred.append((si + 3, pend[0], pend[1]))
                pendq.append((n, g, kb, pti, first, lastkb))
                while deferred and deferred[0][0] <= si:
                    _, fn_, fg_ = deferred.pop(0)
                    finalize(fn_, fg_)
                pump(2)
            for pend in pendq:
                do_pv(*pend)
                if pend[5]:
                    deferred.append((0, pend[0], pend[1]))
            for (_, fn_, fg_) in deferred:
                finalize(fn_, fg_)
            enq_wout(ntl - 1)
            while work:
                assert pump(0)
            tap('mixc1', mixc[1][:].rearrange('p a b -> p (a b)'), [128, 2048], BF16, mixc_r)
            tap('x_c1', xT[:].rearrange('p a b -> p (a b)'), [128, 8 * S], F32, x_res)
            tap('cx_c1', cxT[:].rearrange('p a b -> p (a b)'), [128, 8 * LC], F32, [cx_res])
            PB.close()
            SQ.close()

            PC = Scope(P, nc)
            xm2s = [PC.sb("xm2_%d" % i, [128, 8, 768], BF16) for i in range(2)]
            xm2_rs = [PC.res("xm2_%d" % i) for i in range(2)]
            actb = PC.sb("actb", [128, KF, 768], BF16)
            act_r = PC.res("actb")
            if b == 0 and l == 0:
                mod_layer(1, PC, 2)
            if not last:
                groups = [[("c", 0, 256), ("l", 0, 512)], [("l", 512, 512), ("l", 1024, 256)], [("l", 1280, 512), ("l", 1792, 256)]]
            else:
                groups = [[("l", 0, 512), ("l", 512, 256)], [("l", 768, 512), ("l", 1280, 256)], [("l", 1536, 512)]]
            gsegs = []
            for grp in groups:
                segs = []
                go = 0
                for (sn, s0, ln) in grp:
                    if sn == "c":
                        segs.append((lambda c, a=s0, ln=ln: cxT[:, c, a:a + ln], [cx_res], 4, go, ln))
                    else:
                        rl = [x_res[i] for i in range(s0 // 512, (s0 + ln - 1) // 512 + 1)]
                        segs.append((lambda c, a=s0, ln=ln: xT[:, c, a:a + ln], rl, b, go, ln))
                    go += ln
                gsegs.append(segs)

            def do_norm2(gi):
                xm2_, xm2_r_ = xm2s[gi % 2], xm2_rs[gi % 2]
                for (xf, xr, j, go, ln) in gsegs[gi]:
                    norm(xf, ln, (lambda c, j=j: A12[:, l, 1, j, c:c + 1]), modv(24, j), (lambda c, go=go, ln=ln, xm2_=xm2_: xm2_[:, c, go:go + ln]), xr, [xm2_r_])

            do_norm2(0)
            for gi, segs in enumerate(gsegs):
                xm2, xm2_r = xm2s[gi % 2], xm2_rs[gi % 2]
                pre_sq, pre_mm, pre_tail = [], [], []
                if gi + 1 < len(gsegs):
                    nx2, nx2_r = xm2s[(gi + 1) % 2], xm2_rs[(gi + 1) % 2]
                    for (xf_, xr_, j_, go_, ln_) in gsegs[gi + 1]:
                        a_, b_, c_ = norm_split(xf_, ln_, (lambda c, j_=j_: A12[:, l, 1, j_, c:c + 1]), modv(24, j_),
                                                (lambda c, go_=go_, ln_=ln_, nx2=nx2: nx2[:, c, go_:go_ + ln_]), xr_, [nx2_r])
                        pre_sq += a_
                        pre_mm += b_
                        pre_tail.append(c_)
                pend_mm = []
                for jj in range(11):
                    slot, slot_r = w_next(l, 7 + jj)
                    sv = slot[:, :].rearrange("p (k n) -> p k n", k=8)
                    for (xf, xr, j, go, ln) in segs:
                        for i in range(2):
                            bg = balloc()
                            mm_acc(bg, ln, (lambda k, i=i, sv=sv: sv[:, k, i * 128:(i + 1) * 128]), (lambda k, go=go, ln=ln, xm2=xm2: xm2[:, k, go:go + ln]), 8, [slot_r, xm2_r])
                            bu = balloc()
                            mm_acc(bu, ln, (lambda k, i=i, sv=sv: sv[:, k, 256 + i * 128:256 + (i + 1) * 128]), (lambda k, go=go, ln=ln, xm2=xm2: xm2[:, k, go:go + ln]), 8, [slot_r, xm2_r])
                            sg, sgr = scr()
                            P.op("act", (lambda e, bg=bg, sg=sg, ln=ln: e.activation(out=sg[:, 0:ln], in_=banks[bg][:, 0:ln], func=AF.Silu)),
                                 reads=[bank_r[bg]], writes=[sgr])
                            bfree(bg)
                            P.op("dve", (lambda e, bu=bu, sg=sg, ln=ln, go=go, jj=jj, i=i: e.tensor_tensor(
                                out=actb[:, 2 * jj + i, go:go + ln], in0=banks[bu][:, 0:ln], in1=sg[:, 0:ln], op=ALU.mult)),
                                reads=[bank_r[bu], sgr], writes=[act_r])
                            bfree(bu)
                            if jj >= 5:
                                while pend_mm:
                                    pend_mm.pop(0)()
                                if pre_sq:
                                    pre_sq.pop(0)()
                                    pend_mm.append(pre_mm.pop(0))
                while pre_sq or pend_mm:
                    while pend_mm:
                        pend_mm.pop(0)()
                    if pre_sq:
                        pre_sq.pop(0)()
                        pend_mm.append(pre_mm.pop(0))
                for t_ in pre_tail:
                    t_()
                for o in range(8):
                    slot, slot_r = w_next(l, 18 + o)
                    for (xf, xr, j, go, ln) in segs:
                        bk = balloc()
                        mm_acc(bk, ln, (lambda k, slot=slot: slot[:, k * 128:(k + 1) * 128]), (lambda k, go=go, ln=ln: actb[:, k, go:go + ln]), KF, [slot_r, act_r])
                        P.op("dve", (lambda e, o=o, bk=bk, xf=xf, j=j, ln=ln: e.scalar_tensor_tensor(
                            out=xf(o), in0=banks[bk][:, 0:ln], scalar=modT[:, l, 40 + o, j:j + 1], in1=xf(o), op0=ALU.mult, op1=ALU.add)),
                            reads=[bank_r[bk], mod_res], writes=xr)
                        bfree(bk)
            tap('x_l0', xT[:].rearrange('p a b -> p (a b)'), [128, 8 * S], F32, x_res)
            tap('cx_l0', cxT[:].rearrange('p a b -> p (a b)'), [128, 8 * LC], F32, [cx_res])
            PC.close()

        for l in range(DEPTH):
            run_layer(l)


        PF = Scope(P, nc)
        ost = [PF.sb("ost%d" % i, [128, 8, 512], F32) for i in range(2)]
        ost_r = [PF.res("ost%d" % i) for i in range(2)]
        for t in range(4):
            i = t % 2
            norm((lambda c, t=t: xT[:, c, t * 512:(t + 1) * 512]), 512, lambda c: fg32[:, c:c + 1], None,
                 (lambda c, i=i: ost[i][:, c, :]), [x_res[t]], [ost_r[i]])
            tok = P.dma("pool", "st%d" % i, (lambda e, b=b, t=t, i=i: e.dma_start(
                out=y_d[b].rearrange("p (c s) -> p c s", c=8)[:, :, t * 512:(t + 1) * 512], in_=ost[i][:, :, :])), reads=[ost_r[i]])
            store_tokens.append(tok)
        PF.close()

    for b in range(NB):
        run_sample(b)

    P.wait_tokens("pool", store_tokens)
    P.emit()
    G.close()
    return nc


_CACHE = {}


def kernel(x, c, ctx, c_ctx, w_mod, b_mod, norm1_g, norm2_g, w_in, conv_dw, conv_dw_b, conv_ln_g, conv_ln_b,
           attn_sink, pool_w, pool_scale, w_out, w_ffn_in, w_ffn_out, final_g):
    f = lambda a: np.asarray(a, dtype=np.float32)
    x, c, ctx, c_ctx = f(x), f(c), f(ctx), f(c_ctx)
    wbig = _host_weights(f(w_in), f(w_out), f(w_ffn_in), f(w_ffn_out))
    sp = _host_small(f(b_mod), f(norm1_g), f(norm2_g), f(conv_dw), f(conv_dw_b), f(conv_ln_g), f(conv_ln_b),
                     f(attn_sink), f(pool_w), f(pool_scale), f(final_g))
    ct = _host_consts()
    wm = f(w_mod)
    wmod = np.ascontiguousarray(wm.reshape(DEPTH, 8, 128, 8, 768).transpose(0, 3, 2, 1, 4)).reshape(DEPTH, 8, 128, 8 * 768)
    xT = np.ascontiguousarray(x.reshape(32, S, 8, 128).transpose(0, 3, 2, 1)).reshape(32, 128, 8 * S)
    cxT = np.ascontiguousarray(ctx.reshape(32, LC, 8, 128).transpose(0, 3, 2, 1)).reshape(32, 128, 8 * LC)
    in_maps = []
    for core in range(8):
        bs = slice(core * NB, (core + 1) * NB)
        cc = np.concatenate([c[bs], c_ctx[None, :]], 0)
        cT = np.ascontiguousarray(cc.reshape(5, 8, 128).transpose(2, 1, 0)).reshape(128, 40)
        in_maps.append({"xT": xT[bs], "cxT": cxT[bs], "cT": cT, "wmod": wmod, "sp": sp, "ct": ct, "wbig": wbig})
    if "nc" not in _CACHE:
        _CACHE["nc"] = build_program()
    nc = _CACHE["nc"]
    res = run_bass_kernel_spmd(nc, in_maps, core_ids=list(range(8)))
    yT = np.concatenate([r["yT"] for r in res.results], 0)
    out = np.ascontiguousarray(yT.reshape(32, 128, 8, S).transpose(0, 3, 2, 1)).reshape(32, S, D)
    return out.astype(np.float32)
```
